# Optimizing a Trainium2 kernel written in Bass

```python
import jax, jax.numpy as jnp
from jax import lax
import numpy as np

D_MODEL = 1024
BATCH = 32
SEQ = 256
DEPTH = 2
DEC_BATCH = 8
DEC_SEQ = 1024
PAST_LEN = 256

GRID_W = 64
CONV_DIM = 512
CONV_K = 3
NA_HEADS = 8
NA_HD = 64
NA_WIN_R = 8
NA_WIN_C = 16
MLA_HEADS = 8
MLA_NOPE = 64
MLA_ROPE = 32
MLA_V = 64
Q_LORA = 256
KV_LORA = 128
FFN_DIM = 2816
N_BRANCH = 3
N_MOD = 9
ROPE_BASE = 10000.0
EPS = 1e-6
Q_BLOCK = 128
NEG_INF = -1e30
MLA_SCALE = (MLA_NOPE + MLA_ROPE) ** -0.5
NA_SCALE = NA_HD ** -0.5
IN_SPLITS = (CONV_DIM, CONV_DIM, CONV_DIM,
             NA_HEADS * NA_HD, NA_HEADS * NA_HD, NA_HEADS * NA_HD,
             Q_LORA, KV_LORA, MLA_ROPE,
             D_MODEL, D_MODEL, D_MODEL)
IN_DIM = 3 * CONV_DIM + 3 * NA_HEADS * NA_HD + Q_LORA + KV_LORA + MLA_ROPE + N_BRANCH * D_MODEL

kernel_name = "hybrid_diffusion_conv_na_mla_step"


def rmsnorm(x, g):
    xf = x.astype(jnp.float32)
    y = xf * lax.rsqrt(jnp.mean(xf * xf, axis=-1, keepdims=True) + EPS)
    return (y * g.astype(jnp.float32)).astype(x.dtype)


def modulate(h, shift, scale):
    return h * (1 + scale) + shift


def swiglu(h, w_gate, w_up, w_down):
    return (jax.nn.silu(h @ w_gate) * (h @ w_up)) @ w_down


def split_cols(u, sizes):
    out, off = [], 0
    for n in sizes:
        out.append(u[..., off:off + n])
        off += n
    return out


def to_heads(t, n_heads):
    b, s, _ = t.shape
    return t.reshape(b, s, n_heads, -1).transpose(0, 2, 1, 3)


def from_heads(t):
    b, h, s, d = t.shape
    return t.transpose(0, 2, 1, 3).reshape(b, s, h * d)


def short_conv(u, w, bias):
    s = u.shape[1]
    pad = CONV_K // 2
    up = jnp.pad(u, ((0, 0), (pad, pad), (0, 0)))
    y = bias
    for i in range(CONV_K):
        y = y + up[:, i:i + s] * w[i]
    return y


def axial_rope(x, rows, cols):
    half = MLA_ROPE // 2
    nf = half // 2
    inv = 1.0 / (ROPE_BASE ** (jnp.arange(nf, dtype=jnp.float32) / nf))

    def rot(xh, pos):
        ang = pos.astype(jnp.float32)[:, None] * inv[None, :]
        cos = jnp.cos(ang).astype(x.dtype)
        sin = jnp.sin(ang).astype(x.dtype)
        x1, x2 = xh[..., :nf], xh[..., nf:]
        return jnp.concatenate([x1 * cos - x2 * sin, x1 * sin + x2 * cos], axis=-1)

    return jnp.concatenate([rot(x[..., :half], rows), rot(x[..., half:], cols)], axis=-1)


def attend_blocked(q, k, v, scale):
    b, h, sq, d = q.shape
    nb = sq // Q_BLOCK
    qb = q.reshape(b, h, nb, Q_BLOCK, d).transpose(2, 0, 1, 3, 4)

    def one(qi):
        s = jnp.einsum('bhqd,bhkd->bhqk', qi, k).astype(jnp.float32) * scale
        p = jax.nn.softmax(s, axis=-1).astype(v.dtype)
        return jnp.einsum('bhqk,bhkd->bhqd', p, v)

    o = lax.map(one, qb)
    return o.transpose(1, 2, 0, 3, 4).reshape(b, h, sq, v.shape[-1])


def neighbourhood_attention(q, k, v, k_ctx, v_ctx, rpb):
    b, h, s, hd = q.shape
    rows = s // GRID_W
    kr = min(NA_WIN_R, rows)
    r = jnp.arange(rows)
    col = jnp.arange(GRID_W)
    row_idx = jnp.clip(r - kr // 2, 0, rows - kr)[:, None] + jnp.arange(kr)[None, :]
    col_start = jnp.clip(col - NA_WIN_C // 2, 0, GRID_W - NA_WIN_C)
    col_rel = col[None, :] - col_start[:, None]
    col_in = (col_rel >= 0) & (col_rel < NA_WIN_C)
    dr = row_idx - r[:, None] + (NA_WIN_R - 1)
    dc = jnp.clip(col[None, :] - col[:, None] + (NA_WIN_C - 1), 0, 2 * NA_WIN_C - 2)
    bias = rpb[:, dr[:, None, :, None], dc[None, :, None, :]].astype(jnp.float32)
    qg = q.reshape(b, h, rows, GRID_W, hd)
    kg = k.reshape(b, h, rows, GRID_W, hd)[:, :, row_idx]
    vg = v.reshape(b, h, rows, GRID_W, hd)[:, :, row_idx]
    s_loc = jnp.einsum('bhrqd,bhrikd->bhrqik', qg, kg).astype(jnp.float32) * NA_SCALE + bias[None]
    s_loc = jnp.where(col_in[:, None, :], s_loc, NEG_INF)
    s_loc = s_loc.reshape(b, h, rows, GRID_W, kr * GRID_W)
    s_ctx = jnp.einsum('bhrqd,bhpd->bhrqp', qg, k_ctx).astype(jnp.float32) * NA_SCALE
    prob = jax.nn.softmax(jnp.concatenate([s_loc, s_ctx], axis=-1), axis=-1).astype(v.dtype)
    p_loc = prob[..., :kr * GRID_W].reshape(b, h, rows, GRID_W, kr, GRID_W)
    p_ctx = prob[..., kr * GRID_W:]
    o = (jnp.einsum('bhrqik,bhrikd->bhrqd', p_loc, vg)
         + jnp.einsum('bhrqp,bhpd->bhrqd', p_ctx, v_ctx))
    return o.reshape(b, h, s, hd)


def mla_expand(c_kv, k_r, w_ukv):
    kv = to_heads(c_kv @ w_ukv, MLA_HEADS)
    k_nope, v = kv[..., :MLA_NOPE], kv[..., MLA_NOPE:]
    b, h, s, _ = k_nope.shape
    k_rope = jnp.broadcast_to(k_r[:, None], (b, h, s, MLA_ROPE))
    return jnp.concatenate([k_nope, k_rope], axis=-1), v


def layer(x, mod, p, ctx=None):
    s = x.shape[1]
    m = lambda i: mod[:, :, i]
    ng = p['norm_g']
    h = modulate(rmsnorm(x, ng[0]), m(0), m(1))
    x = x + 0.5 * m(2) * swiglu(h, p['w_ffn1_gate'], p['w_ffn1_up'], p['w_ffn1_down'])
    h = modulate(rmsnorm(x, ng[1]), m(3), m(4))
    (b_g, c_g, x_c, q_na, k_na, v_na, c_q, c_kv, k_r,
     g_conv, g_na, g_mla) = split_cols(h @ p['w_in'], IN_SPLITS)
    y_conv = b_g * short_conv(c_g * x_c, p['conv_w'], p['conv_b'])
    q_na, k_na, v_na = to_heads(q_na, NA_HEADS), to_heads(k_na, NA_HEADS), to_heads(v_na, NA_HEADS)
    q_m = to_heads(rmsnorm(c_q, p['mla_qnorm']) @ p['w_uq'], MLA_HEADS)
    c_kv = rmsnorm(c_kv, p['mla_kvnorm'])
    if ctx is None:
        o_na = attend_blocked(q_na, k_na, v_na, NA_SCALE)
        k_m, v_m = mla_expand(c_kv, k_r, p['w_ukv'])
        o_m = attend_blocked(q_m, k_m, v_m, MLA_SCALE)
        new = (k_na, v_na, c_kv, k_r)
    else:
        ck_na, cv_na, cc_kv, ck_r = ctx
        o_na = neighbourhood_attention(q_na, k_na, v_na, ck_na, cv_na, p['na_rpb'])
        t = jnp.arange(s)
        rows, cols = t // GRID_W, t % GRID_W
        q_m = jnp.concatenate([q_m[..., :MLA_NOPE], axial_rope(q_m[..., MLA_NOPE:], rows, cols)], axis=-1)
        k_lat, v_lat = mla_expand(c_kv, axial_rope(k_r, rows, cols), p['w_ukv'])
        k_ctx, v_ctx = mla_expand(cc_kv, ck_r, p['w_ukv'])
        o_m = attend_blocked(q_m, jnp.concatenate([k_ctx, k_lat], axis=2),
                             jnp.concatenate([v_ctx, v_lat], axis=2), MLA_SCALE)
        new = None
    z = (jax.nn.sigmoid(g_conv) * (y_conv @ p['w_conv_out'])
         + jax.nn.sigmoid(g_na) * (from_heads(o_na) @ p['w_na_out'])
         + jax.nn.sigmoid(g_mla) * (from_heads(o_m) @ p['w_mla_out']))
    x = x + m(5) * (z @ p['w_o'])
    h = modulate(rmsnorm(x, ng[2]), m(6), m(7))
    x = x + 0.5 * m(8) * swiglu(h, p['w_ffn2_gate'], p['w_ffn2_up'], p['w_ffn2_down'])
    return x, new


def setup_inputs(seed: int = 0) -> dict:
    key = jax.random.key(seed)
    ks = jax.random.split(key, 32)
    f32 = jnp.float32

    def nrm(k, shape, scale):
        return jax.random.normal(k, shape, f32) * scale

    D, L, F = D_MODEL, DEPTH, FFN_DIM
    return {
        'x_prompt': nrm(ks[0], (BATCH, SEQ, D), 1.0),
        'x_sample': nrm(ks[1], (DEC_BATCH, DEC_SEQ, D), 1.0),
        'cache_na_k': nrm(ks[2], (DEC_BATCH, L, NA_HEADS, PAST_LEN, NA_HD), 1.0),
        'cache_na_v': nrm(ks[3], (DEC_BATCH, L, NA_HEADS, PAST_LEN, NA_HD), 1.0),
        'cache_mla_ckv': nrm(ks[4], (DEC_BATCH, L, PAST_LEN, KV_LORA), 1.0),
        'cache_mla_krope': nrm(ks[5], (DEC_BATCH, L, PAST_LEN, MLA_ROPE), 1.0),
        'c': nrm(ks[6], (DEC_BATCH, D), 1.0),
        'c_ctx': nrm(ks[7], (D,), 1.0),
        'w_ada': nrm(ks[8], (L, D, N_MOD * D), 0.3 * D ** -0.5),
        'b_ada': nrm(ks[9], (L, N_MOD * D), 0.01),
        'norm_g': 1.0 + nrm(ks[10], (L, 3, D), 0.02),
        'w_ffn1_gate': nrm(ks[11], (L, D, F), D ** -0.5),
        'w_ffn1_up': nrm(ks[12], (L, D, F), D ** -0.5),
        'w_ffn1_down': nrm(ks[13], (L, F, D), F ** -0.5),
        'w_ffn2_gate': nrm(ks[14], (L, D, F), D ** -0.5),
        'w_ffn2_up': nrm(ks[15], (L, D, F), D ** -0.5),
        'w_ffn2_down': nrm(ks[16], (L, F, D), F ** -0.5),
        'w_in': nrm(ks[17], (L, D, IN_DIM), D ** -0.5),
        'conv_w': nrm(ks[18], (L, CONV_K, CONV_DIM), CONV_K ** -0.5),
        'conv_b': nrm(ks[19], (L, CONV_DIM), 0.01),
        'na_rpb': nrm(ks[20], (L, NA_HEADS, 2 * NA_WIN_R - 1, 2 * NA_WIN_C - 1), 0.1),
        'mla_qnorm': 1.0 + nrm(ks[21], (L, Q_LORA), 0.02),
        'w_uq': nrm(ks[22], (L, Q_LORA, MLA_HEADS * (MLA_NOPE + MLA_ROPE)), Q_LORA ** -0.5),
        'mla_kvnorm': 1.0 + nrm(ks[23], (L, KV_LORA), 0.02),
        'w_ukv': nrm(ks[24], (L, KV_LORA, MLA_HEADS * (MLA_NOPE + MLA_V)), KV_LORA ** -0.5),
        'w_conv_out': nrm(ks[25], (L, CONV_DIM, D), CONV_DIM ** -0.5),
        'w_na_out': nrm(ks[26], (L, NA_HEADS * NA_HD, D), (NA_HEADS * NA_HD) ** -0.5),
        'w_mla_out': nrm(ks[27], (L, MLA_HEADS * MLA_V, D), (MLA_HEADS * MLA_V) ** -0.5),
        'w_o': nrm(ks[28], (L, D, D), D ** -0.5),
        'final_g': 1.0 + nrm(ks[29], (D,), 0.02),
    }


def reference(x_prompt, x_sample, cache_na_k, cache_na_v, cache_mla_ckv, cache_mla_krope, c, c_ctx,
              w_ada, b_ada, norm_g, w_ffn1_gate, w_ffn1_up, w_ffn1_down, w_ffn2_gate, w_ffn2_up, w_ffn2_down,
              w_in, conv_w, conv_b, na_rpb, mla_qnorm, w_uq, mla_kvnorm, w_ukv,
              w_conv_out, w_na_out, w_mla_out, w_o, final_g):
    xp, xs = x_prompt, x_sample
    new_k, new_v, new_ckv, new_kr = [], [], [], []
    for l in range(DEPTH):
        p = dict(norm_g=norm_g[l], w_ffn1_gate=w_ffn1_gate[l], w_ffn1_up=w_ffn1_up[l], w_ffn1_down=w_ffn1_down[l],
                 w_ffn2_gate=w_ffn2_gate[l], w_ffn2_up=w_ffn2_up[l], w_ffn2_down=w_ffn2_down[l],
                 w_in=w_in[l], conv_w=conv_w[l], conv_b=conv_b[l], na_rpb=na_rpb[l],
                 mla_qnorm=mla_qnorm[l], w_uq=w_uq[l], mla_kvnorm=mla_kvnorm[l], w_ukv=w_ukv[l],
                 w_conv_out=w_conv_out[l], w_na_out=w_na_out[l], w_mla_out=w_mla_out[l], w_o=w_o[l])
        mod_ctx = (jax.nn.silu(c_ctx) @ w_ada[l] + b_ada[l]).reshape(1, 1, N_MOD, D_MODEL)
        mod_lat = (jax.nn.silu(c) @ w_ada[l] + b_ada[l]).reshape(-1, 1, N_MOD, D_MODEL)
        xp, (k_na, v_na, ckv, kr) = layer(xp, mod_ctx, p)
        xs, _ = layer(xs, mod_lat, p, (cache_na_k[:, l], cache_na_v[:, l], cache_mla_ckv[:, l], cache_mla_krope[:, l]))
        new_k.append(k_na)
        new_v.append(v_na)
        new_ckv.append(ckv)
        new_kr.append(kr)
    y_prompt = rmsnorm(xp, final_g)
    y_sample = rmsnorm(xs, final_g)
    return (y_prompt, y_sample, jnp.stack(new_k, axis=1), jnp.stack(new_v, axis=1),
            jnp.stack(new_ckv, axis=1), jnp.stack(new_kr, axis=1))
```

```python
import contextlib
import numpy as np
import concourse.bass as bass
import concourse.mybir as mybir
from concourse.bass_utils import run_bass_kernel_spmd

F32 = mybir.dt.float32
BF16 = mybir.dt.bfloat16
AF = mybir.ActivationFunctionType
ALU = mybir.AluOpType

D = 1024
FF = 2816
L = 2
NTOK = 2048
GT = 1024
TT = 512
PAST = 256
IN_DIM = 6560
EPS = 1e-6
NA_SCALE = 64 ** -0.5
MLA_SCALE = 96 ** -0.5
NEG = -30000.0
ENGS = ("pe", "act", "dve", "pool", "sp")

SP_C = 0
SP_NG = 16
SP_BADA = 64
SP_CW = 208
SP_CB = 232
SP_QN = 240
SP_KVN = 244
SP_N = 246


class Op:
    __slots__ = ("eng", "fn", "deps", "is_dma", "tok", "signal", "dma_need")

    def __init__(self, eng, fn, deps, is_dma):
        self.eng = eng
        self.fn = fn
        self.deps = deps
        self.is_dma = is_dma
        self.tok = None
        self.signal = False
        self.dma_need = {}


class Prog:
    def __init__(self):
        self.ops = {e: [] for e in ENGS}
        self.last_w = {}
        self.readers = {}
        self.dma_sems = {}
        self.n_ops = 0

    def add(self, eng, fn, reads=(), writes=(), dma_sem=None):
        deps = set()
        for k in reads:
            w = self.last_w.get(k)
            if w is not None:
                deps.add(w)
        for k in writes:
            w = self.last_w.get(k)
            if w is not None:
                deps.add(w)
            for r in self.readers.get(k, ()):
                deps.add(r)
        if eng == "pe":
            deps = {d for d in deps if d.eng != "pe"}
        op = Op(eng, fn, deps, dma_sem is not None)
        for d in deps:
            if d.is_dma:
                op.dma_need[d.tok[0]] = self.dma_sems[d.tok[0]]
        if dma_sem is not None:
            c = self.dma_sems.get(dma_sem, 0) + 16
            self.dma_sems[dma_sem] = c
            op.tok = (dma_sem, c)
        for k in reads:
            lst = self.readers.setdefault(k, [])
            if not op.is_dma:
                lst[:] = [r for r in lst if r.is_dma or r.eng != op.eng]
            lst.append(op)
        for k in writes:
            self.last_w[k] = op
            self.readers[k] = []
        self.ops[eng].append(op)
        self.n_ops += 1
        return op

    def emit(self, nc, st):
        for e in ENGS:
            for op in self.ops[e]:
                for d in op.deps:
                    if not d.is_dma:
                        d.signal = True
        for e in ENGS:
            c = 0
            for op in self.ops[e]:
                if op.is_dma:
                    continue
                if op.signal:
                    c += 1
                    op.tok = ("eng_" + e, c)
        sem_names = ["eng_" + e for e in ENGS] + sorted(self.dma_sems.keys())
        sems = {n: st.enter_context(nc.semaphore(n)) for n in sem_names}
        block = st.enter_context(nc.Block())
        prog = self

        def run(e, eng):
            waited = {}
            for op in prog.ops[e]:
                need = {}
                for d in op.deps:
                    if d.is_dma:
                        continue
                    s, v = d.tok
                    if v > need.get(s, 0):
                        need[s] = v
                for s, v in op.dma_need.items():
                    if v > need.get(s, 0):
                        need[s] = v
                for s, v in need.items():
                    if waited.get(s, 0) < v:
                        eng.wait_ge(sems[s], v)
                        waited[s] = v
                ins = op.fn(eng)
                if op.is_dma:
                    ins.then_inc(sems[op.tok[0]], 16)
                elif op.signal:
                    ins.then_inc(sems[op.tok[0]], 1)
            if e == "sp":
                for s, c in prog.dma_sems.items():
                    if waited.get(s, 0) < c:
                        eng.wait_ge(sems[s], c)

        @block.tensor
        def _(eng):
            run("pe", eng)

        @block.scalar
        def _(eng):
            run("act", eng)

        @block.vector
        def _(eng):
            run("dve", eng)

        @block.gpsimd
        def _(eng):
            run("pool", eng)

        @block.sync
        def _(eng):
            run("sp", eng)


def build_program(stage=None):
    nc = bass.Bass("TRN2", target_bir_lowering=False)
    P = Prog()

    def din(name, shape, dt=F32):
        return nc.dram_tensor(name, list(shape), dt, kind="ExternalInput").ap()

    def dout(name, shape, dt=F32):
        return nc.dram_tensor(name, list(shape), dt, kind="ExternalOutput").ap()

    x_d = din("x_tok", [NTOK, D])
    sp_d = din("sp", [128, SP_N])
    fg_d = din("final_g_b", [128, D])
    kvb_d = din("kvnorm_b", [128, L * 128])
    cnk_d = din("c_na_k", [L, 8, PAST, 64])
    cnv_d = din("c_na_v", [L, 8, PAST, 64])
    cck_d = din("c_ckv", [L, PAST, 128])
    ckr_d = din("c_kr", [L, PAST, 32])
    gm_d = din("gm", [L, 8, 64, 960])
    cos_d = din("cos_t", [128, GT])
    sin_d = din("sin_t", [128, GT])
    w_ada_d = din("w_ada", [L, D, 9 * D])
    wf_d = {}
    for nm, shp in (("w_ffn1_gate", [L, D, FF]), ("w_ffn1_up", [L, D, FF]), ("w_ffn1_down", [L, FF, D]),
                    ("w_ffn2_gate", [L, D, FF]), ("w_ffn2_up", [L, D, FF]), ("w_ffn2_down", [L, FF, D])):
        wf_d[nm] = din(nm, shp)
    w_in_d = din("w_in", [L, D, IN_DIM])
    w_uq_d = din("w_uq", [L, 256, 768])
    w_ukv_d = din("w_ukv", [L, 128, 1024])
    wco_d = din("w_conv_out", [L, 512, D])
    wno_d = din("w_na_out", [L, 512, D])
    wmo_d = din("w_mla_out", [L, 512, D])
    wo_d = din("w_o", [L, D, D])

    y_d = dout("y", [NTOK, D])
    nk_d = dout("new_k", [4, L, 8, 256, 64])
    nv_d = dout("new_v", [4, L, 8, 256, 64])
    nckv_d = dout("new_ckv", [4, L, 256, 128])
    nkr_d = dout("new_kr", [4, L, 256, 32])

    with contextlib.ExitStack() as st:
        def sb(name, shape, dt):
            return st.enter_context(nc.sbuf_tensor(name, list(shape), dt))

        xT = sb("xT", [128, 8, NTOK], F32)
        hT = sb("hT", [128, 8, NTOK], BF16)
        NATOM = 52
        AR = sb("arena", [128, NATOM * 512], BF16)
        oTn = sb("oTn", [128, 4, GT], BF16)
        oTm = sb("oTm", [128, 4, GT], BF16)
        NPT = 4
        ptr = sb("ptr", [128, NPT, 512], BF16)
        bufA = sb("bufA", [128, 1024], F32)
        bufB = sb("bufB", [128, 1024], F32)
        NSCR = 5
        scr_t = sb("scr", [128, NSCR, 512], F32)
        xs = sb("xs", [128, 2, 1024], F32)
        NSTG = 2
        stg = sb("stg", [128, NSTG, 512], F32)
        spt = sb("spt", [128, SP_N], F32)
        modTs = [sb("modT%d" % i, [128, 72, 2], F32) for i in range(2)]
        Ascs = [sb("Asc%d" % i, [128, 3, 8, 2], F32) for i in range(2)]
        HGs = [sb("HG%d" % i, [128, 3, 8, 2], F32) for i in range(2)]
        wada = sb("wada", [128, 2, 1024], BF16)
        LC = [0]
        csil = sb("csil", [128, 16], BF16)
        ident = sb("ident", [128, 128], F32)
        ones = sb("ones", [128, 128], BF16)
        kvb = sb("kvb", [128, L * 128], F32)
        sm = sb("sm", [128, 16], F32)
        epsb = sb("epsb", [128, 1], F32)
        ps = st.enter_context(nc.psum_tensor("ps", [128, 8, 512], F32))
        print("sbuf bytes remaining", nc.sbuf_bytes_remaining, flush=True)

        cnt = {"bank": 0, "scr": 0, "pt": 0, "stg": 0, "stgx": 0, "sm": 0, "q": 0}

        reserved = set()

        def bank():
            while True:
                b = cnt["bank"] % 8
                cnt["bank"] += 1
                if b not in reserved:
                    return b

        def scr():
            i = cnt["scr"] % NSCR
            cnt["scr"] += 1
            return scr_t[:, i, :], ("scr", i)

        def ptslot():
            i = cnt["pt"] % NPT
            cnt["pt"] += 1
            return ptr[:, i, :], ("pt", i)

        def stgslot(ext=False):
            if ext:
                i = cnt["stgx"] % (NSTG + 2)
                cnt["stgx"] += 1
                if i >= NSTG:
                    return xs[:, i - NSTG, 0:512], "xs%d" % (i - NSTG), "st_%d" % i
                return stg[:, i, :], ("stg", i), "st_%d" % i
            i = cnt["stg"] % NSTG
            cnt["stg"] += 1
            return stg[:, i, :], ("stg", i), "st_%d" % i

        def smslot():
            i = cnt["sm"] % 16
            cnt["sm"] += 1
            return sm[:, i:i + 1], ("sm", i)

        def dmaq():
            return "sp"

        def A(off, n):
            keys = [("A", a) for a in range(off // 512, (off + n - 1) // 512 + 1)]
            return AR[:, off:off + n], keys

        def PSK(b):
            return ("ps", b)

        def mm(out, lhsT, rhs, start, stop, reads, writes):
            P.add("pe", lambda e: e.matmul(out, lhsT=lhsT, rhs=rhs, start=start, stop=stop, skip_group_check=True),
                  reads=reads, writes=writes)

        def tr(out, in_, reads, writes):
            P.add("pe", lambda e: e.transpose(out=out, in_=in_, identity=ident[:]), reads=list(reads) + ["ident"], writes=writes)

        def actf(out, in_, func, reads, writes, scale=1.0, bias=None, accum_out=None):
            def f(e):
                kw = {}
                if bias is not None:
                    kw["bias"] = bias
                if accum_out is not None:
                    kw["accum_out"] = accum_out
                return e.activation(out=out, in_=in_, func=func, scale=scale, **kw)
            P.add("act", f, reads=reads, writes=writes)

        def tt(eng, out, in0, in1, op, reads, writes):
            P.add(eng, lambda e: e.tensor_tensor(out=out, in0=in0, in1=in1, op=op), reads=reads, writes=writes)

        def stt(eng, out, in0, scalar, in1, op0, op1, reads, writes):
            P.add(eng, lambda e: e.scalar_tensor_tensor(out=out, in0=in0, scalar=scalar, in1=in1, op0=op0, op1=op1),
                  reads=reads, writes=writes)

        def ts(eng, out, in0, s1, op0, reads, writes, s2=None, op1=None):
            def f(e):
                if op1 is None:
                    return e.tensor_scalar(out=out, in0=in0, scalar1=s1, scalar2=None, op0=op0)
                return e.tensor_scalar(out=out, in0=in0, scalar1=s1, scalar2=s2, op0=op0, op1=op1)
            P.add(eng, f, reads=reads, writes=writes)

        def cp(eng, out, in_, reads, writes):
            if eng == "act":
                actf(out, in_, AF.Identity, reads, writes)
            else:
                P.add(eng, lambda e: e.tensor_copy(out=out, in_=in_), reads=reads, writes=writes)

        def recip(out, in_, reads, writes):
            P.add("dve", lambda e: e.reciprocal(out=out, in_=in_), reads=reads, writes=writes)

        def dma(eng, out, in_, reads, writes, sem):
            P.add(eng, lambda e: e.dma_start(out=out, in_=in_), reads=reads, writes=writes, dma_sem=sem)

        def GBK(i):
            return [("gb", i)] + ([("gb1h", 0), ("gb1h", 1)] if i == 1 else [("gbh", 0), ("gbh", 1)])

        marks = []

        def mark(label):
            marks.append((label, len(P.ops["pe"])))

        def XK(dc, t):
            return ("x", dc, t)

        def HK(kc, t):
            return ("h", kc, t)

        P.add("pool", lambda e: e.memset(ident[:], 0.0), writes=["ident"])
        P.add("pool", lambda e: e.affine_select(out=ident[:], in_=ident[:], pattern=[[-1, 128]], compare_op=ALU.not_equal,
                                                fill=1.0, base=0, channel_multiplier=1), reads=["ident"], writes=["ident"])
        P.add("dve", lambda e: e.memset(ones[:], 1.0), writes=["ones"])
        P.add("dve", lambda e: e.memset(epsb[:], EPS), writes=["epsb"])
        P.add("dve", lambda e: e.memset(xs[:], 0.0), writes=["xs0", "xs1"])
        dma("sp", spt[:], sp_d, [], ["spt"], "ld_c")
        dma("sp", kvb[:], kvb_d, [], ["kvb"], "ld_c")
        actf(csil[:], spt[:, SP_C:SP_C + 16], AF.Silu, ["spt"], ["csil"])

        for t in range(4):
            banks = [bank() for _ in range(8)]
            for qq in range(4):
                q = t * 4 + qq
                lslots = [(xs[:, 0, :], ["xs0"], "ld_x0"), (xs[:, 1, :], ["xs1"], "ld_x1"),
                          (bufA[:, :], GBK(0), "ld_g0"), (bufB[:, :], GBK(1), "ld_g1")]
                lv, lk, lsem = lslots[q % 4]
                dma("sp", lv, x_d[q * 128:(q + 1) * 128, :], [], lk, lsem)
                for dc in range(8):
                    tr(ps[:, banks[dc], qq * 128:(qq + 1) * 128], lv[:, dc * 128:(dc + 1) * 128],
                       lk, [PSK(banks[dc])])
            for dc in range(8):
                cp("dve" if dc % 2 == 0 else "act", xT[:, dc, t * TT:(t + 1) * TT], ps[:, banks[dc], :],
                   [PSK(banks[dc])], [XK(dc, t)])

        W_OFF = [0, 12 * 512, 24 * 512]
        ACT_OFF = [36 * 512, 44 * 512]

        def mod_items(l):
            par = l % 2
            modT, Asc, HG = modTs[par], Ascs[par], HGs[par]
            items = []
            state = {"bank": None}

            def load(j):
                sl = j % 2
                w3 = wada[:, sl, :].rearrange("p (kc n) -> p kc n", n=128)
                P.add("pool", lambda e: e.dma_start(
                    out=w3, in_=w_ada_d[l, :, j * 128:(j + 1) * 128].rearrange("(kc p) n -> p kc n", p=128)),
                    writes=[("wada", sl)], dma_sem="wa%d" % sl)

            def chunk(j):
                def f():
                    if j == 0:
                        load(0)
                        load(1)
                    sl = j % 2
                    w3 = wada[:, sl, :].rearrange("p (kc n) -> p kc n", n=128)
                    jj = j % 6
                    if jj == 0:
                        state["bank"] = bank()
                        reserved.add(state["bank"])
                    b = state["bank"]
                    for kc in range(8):
                        mm(ps[:, b, jj * 2:jj * 2 + 2], w3[:, kc, :], csil[:, kc * 2:kc * 2 + 2],
                           kc == 0, kc == 7, [("wada", sl), "csil"], [PSK(b)])
                    if j + 2 < 72:
                        load(j + 2)
                    if jj == 5:
                        blk = j // 6
                        for m in range(2):
                            psv = ps[:, b, 0:12].rearrange("p (j m) -> p j m", m=2)[:, :, m]
                            tt("dve", modT[:, blk * 6:(blk + 1) * 6, m], psv,
                               spt[:, SP_BADA + l * 72 + blk * 6: SP_BADA + l * 72 + blk * 6 + 6], ALU.add,
                               [PSK(b), "spt"], [("mod", blk, par)])
                        reserved.discard(b)
                return f
            for j in range(72):
                items.append(chunk(j))

            def fin():
                allmod = [("mod", blk, par) for blk in range(12)]
                for i3 in range(3):
                    for m in range(2):
                        j0 = (3 * i3 + 1) * 8
                        stt("dve", Asc[:, i3, :, m], modT[:, j0:j0 + 8, m], 1.0,
                            spt[:, SP_NG + (l * 3 + i3) * 8: SP_NG + (l * 3 + i3) * 8 + 8], ALU.add, ALU.mult,
                            allmod + ["spt"], ["Asc%d" % par])
                        j1 = (3 * i3 + 2) * 8
                        ts("dve", HG[:, i3, :, m], modT[:, j1:j1 + 8, m], 0.5, ALU.mult, allmod, ["HG%d" % par])
            items.append(fin)
            return items

        def mod_fin(l, i3s):
            par = l % 2
            modT, Asc, HG = modTs[par], Ascs[par], HGs[par]

            def fin():
                allmod = [("mod", blk, par) for blk in range(12)]
                for i3 in i3s:
                    for m in range(2):
                        j0 = (3 * i3 + 1) * 8
                        stt("dve", Asc[:, i3, :, m], modT[:, j0:j0 + 8, m], 1.0,
                            spt[:, SP_NG + (l * 3 + i3) * 8: SP_NG + (l * 3 + i3) * 8 + 8], ALU.add, ALU.mult,
                            allmod + ["spt"], ["Asc%d" % par])
                        j1 = (3 * i3 + 2) * 8
                        ts("dve", HG[:, i3, :, m], modT[:, j1:j1 + 8, m], 0.5, ALU.mult, allmod, ["HG%d" % par])
            return fin

        def mod_block_a(l, j, w3, wkeys, W=512):
            b = bank()
            for kc in range(8):
                mm(ps[0:2, b, 0:W], csil[:, kc * 2:kc * 2 + 2], w3[:, kc, :], kc == 0, kc == 7, wkeys + ["csil"], [PSK(b)])
            rb, rbk = bufA[0:2, (j % 2) * 512:(j % 2) * 512 + W], ("gbh", j % 2)
            cp("act", rb, ps[0:2, b, 0:W], [PSK(b), ("gb", 0)], [rbk])

        def mod_block_b(l, j, W=512):
            par = l % 2
            modT = modTs[par]
            rb, rbk = bufA[0:2, (j % 2) * 512:(j % 2) * 512 + W], ("gbh", j % 2)
            nj = W // 128
            b2 = bank()
            for jj in range(nj):
                P.add("pe", lambda e, jj=jj, b2=b2, rb=rb: e.transpose(out=ps[:, b2, jj * 2:jj * 2 + 2],
                                                                       in_=rb[:, jj * 128:(jj + 1) * 128], identity=ident[0:2, 0:2]),
                      reads=[rbk, "ident"], writes=[PSK(b2)])
            for m in range(2):
                psv = ps[:, b2, 0:2 * nj].rearrange("p (j m) -> p j m", m=2)[:, :, m]
                tt("dve", modT[:, j * nj:(j + 1) * nj, m], psv,
                   spt[:, SP_BADA + l * 72 + j * nj: SP_BADA + l * 72 + (j + 1) * nj], ALU.add,
                   [PSK(b2), "spt"], [("mod", (j * nj) // 6, par), ("mod", (j * nj + nj - 1) // 6, par)])

        def mod_items_xs(l, j_from=0):
            items = []

            def wview(j):
                xb = xs[:, j % 2, :].bitcast(BF16)
                return xb.rearrange("p (kc n) -> p kc n", n=256), [XSK(j % 2)]

            def load(j):
                w3, wk = wview(j)
                P.add("pool", lambda e: e.dma_start(
                    out=w3, in_=w_ada_d[l, :, j * 256:(j + 1) * 256].rearrange("(kc p) n -> p kc n", p=128)),
                    writes=wk, dma_sem="wa%d" % (j % 2))

            def blockf(j):
                def f():
                    if j == j_from:
                        load(j)
                        if j + 1 < 36:
                            load(j + 1)
                    else:
                        mod_block_b(l, j - 1, 256)
                    w3, wk = wview(j)
                    mod_block_a(l, j, w3, wk, 256)
                    if j + 2 < 36:
                        load(j + 2)
                return f
            for j in range(j_from, 36):
                items.append(blockf(j))
            fin_ = mod_fin(l, (0, 1, 2) if j_from == 0 else (1, 2))

            def last():
                mod_block_b(l, 35, 256)
                fin_()
            items.append(last)
            return items

        def mod_compute(l, nblk=18):
            if stage is not None:
                for it in mod_items(l):
                    it()
                return
            for j in range(nblk):
                s_ = j % 3
                wv, wk = A(W_OFF[s_], 4096)
                w3 = wv.rearrange("p (kc n) -> p kc n", n=512)
                P.add("pool", lambda e, w3=w3, j=j: e.dma_start(
                    out=w3, in_=w_ada_d[l, :, j * 512:(j + 1) * 512].rearrange("(kc p) n -> p kc n", p=128)),
                    writes=wk, dma_sem="wf%d" % s_)
                mod_block_a(l, j, w3, wk)
                if j > 0:
                    mod_block_b(l, j - 1)
            mod_block_b(l, nblk - 1)
            mod_fin(l, (0, 1, 2) if nblk == 18 else (0,))()

        def norm(i3, tiles=(0, 1, 2, 3)):
            par = LC[0]
            modT, Asc = modTs[par], Ascs[par]
            nb = {}
            for t in tiles:
                b = bank()
                reserved.add(b)
                nb[t] = b
                for dc in range(8):
                    sq, sqk = ptslot()
                    if dc % 2 == 0:
                        tt("pool", sq, xT[:, dc, t * TT:(t + 1) * TT], xT[:, dc, t * TT:(t + 1) * TT], ALU.mult, [XK(dc, t)], [sqk])
                    else:
                        actf(sq, xT[:, dc, t * TT:(t + 1) * TT], AF.Square, [XK(dc, t)], [sqk])
                    mm(ps[:, b, :], ones[:], sq, dc == 0, dc == 7, [sqk, "ones"], [PSK(b)])
            for t in tiles:
                m = t // 2
                b = nb[t]
                rs, rsk = bufB[:, (t % 2) * TT:(t % 2 + 1) * TT], ("gb1h", t % 2)
                actf(rs, ps[:, b, :], AF.Sqrt, [PSK(b), "epsb"], [rsk], scale=1.0 / D, bias=epsb[:])
                reserved.discard(b)
                recip(rs, rs, [rsk], [rsk])
                for dc in range(8):
                    tmp, tk = scr()
                    tt("dve", tmp, xT[:, dc, t * TT:(t + 1) * TT], rs, ALU.mult, [XK(dc, t), rsk], [tk])
                    actf(hT[:, dc, t * TT:(t + 1) * TT], tmp, AF.Identity, [tk, "Asc%d" % par, ("mod", (3 * i3 * 8 + dc) // 6, par)], [HK(dc, t)],
                         scale=Asc[:, i3, dc, m:m + 1], bias=modT[:, 3 * i3 * 8 + dc, m:m + 1])

        def ffn(l, which, extra=None):
            par = LC[0]
            HG = HGs[par]
            HGK = "HG%d" % par
            wg_d = wf_d["w_ffn%d_gate" % which][l]
            wu_d = wf_d["w_ffn%d_up" % which][l]
            wd_d = wf_d["w_ffn%d_down" % which][l]
            i3 = 0 if which == 1 else 2
            NG = 11
            ex = list(extra) if extra else []
            per = (len(ex) + NG - 1) // NG if ex else 0
            tickn = [0]

            def views(g):
                s = g % 3
                wgv, wgk = A(W_OFF[s], 2048)
                wuv, wuk = A(W_OFF[s] + 2048, 2048)
                wdv, wdk = A(W_OFF[s] + 4096, 2048)
                return (wgv.rearrange("p (kc n) -> p kc n", n=256), wgk,
                        wuv.rearrange("p (kc n) -> p kc n", n=256), wuk,
                        wdv.rearrange("p (j n) -> p j n", n=1024), wdk, s)

            def load(g):
                wg3, wgk, wu3, wuk, wd3, wdk, s = views(g)
                f0 = g * 256
                P.add("pool", lambda e: e.dma_start(out=wg3, in_=wg_d[:, f0:f0 + 256].rearrange("(kc p) n -> p kc n", p=128)),
                      writes=wgk, dma_sem="wf%d" % s)
                P.add("pool", lambda e: e.dma_start(out=wu3, in_=wu_d[:, f0:f0 + 256].rearrange("(kc p) n -> p kc n", p=128)),
                      writes=wuk, dma_sem="wf%d" % s)
                P.add("pool", lambda e: e.dma_start(out=wd3, in_=wd_d[f0:f0 + 256, :].rearrange("(j p) n -> p j n", p=128)),
                      writes=wdk, dma_sem="wf%d" % s)

            def actview(g, j, t):
                sa = g % 2
                return A(ACT_OFF[sa] + j * NTOK + t * TT, TT)

            def gu_unit(g, j, t):
                wg3, wgk, wu3, wuk, wd3, wdk, s = views(g)
                bg, bu = bank(), bank()
                for kc in range(8):
                    mm(ps[:, bg, :], wg3[:, kc, j * 128:(j + 1) * 128], hT[:, kc, t * TT:(t + 1) * TT],
                       kc == 0, kc == 7, wgk + [HK(kc, t)], [PSK(bg)])
                for kc in range(8):
                    mm(ps[:, bu, :], wu3[:, kc, j * 128:(j + 1) * 128], hT[:, kc, t * TT:(t + 1) * TT],
                       kc == 0, kc == 7, wuk + [HK(kc, t)], [PSK(bu)])
                sg, sgk = scr()
                actf(sg, ps[:, bg, :], AF.Silu, [PSK(bg)], [sgk])
                av, ak = actview(g, j, t)
                tt("dve", av, ps[:, bu, :], sg, ALU.mult, [PSK(bu), sgk], ak)

            def down_unit(g, t, dc):
                wg3, wgk, wu3, wuk, wd3, wdk, s = views(g)
                m = t // 2
                b = bank()
                for j in range(2):
                    av, ak = actview(g, j, t)
                    mm(ps[:, b, :], wd3[:, j, dc * 128:(dc + 1) * 128], av, j == 0, j == 1, wdk + ak, [PSK(b)])
                xv = xT[:, dc, t * TT:(t + 1) * TT]
                if dc % 2 == 0:
                    stt("dve", xv, ps[:, b, :], HG[:, i3, dc, m:m + 1], xv, ALU.mult, ALU.add,
                        [PSK(b), HGK, XK(dc, t)], [XK(dc, t)])
                else:
                    tmp, tk = scr()
                    actf(tmp, ps[:, b, :], AF.Identity, [PSK(b), HGK], [tk], scale=HG[:, i3, dc, m:m + 1])
                    tt("pool", xv, xv, tmp, ALU.add, [tk, XK(dc, t)], [XK(dc, t)])

            load(0)
            ucount = [0]
            every = max(1, (NG * 8 - 4) // len(ex)) if ex else 0
            for g in range(NG):
                if g + 1 < NG:
                    load(g + 1)
                dunits = [(t, dc) for t in range(4) for dc in range(8)] if g >= 1 else []
                for j in range(2):
                    for t in range(4):
                        gu_unit(g, j, t)
                        ucount[0] += 1
                        if ex and ucount[0] % every == 0:
                            ex.pop(0)()
                        for _ in range(4):
                            if dunits:
                                down_unit(g - 1, *dunits.pop(0))
            for t in range(4):
                for dc in range(8):
                    down_unit(NG - 1, t, dc)
            while ex:
                ex.pop(0)()

        NRING = 6
        WUQ = 12 * 512
        WUQP = 15 * 512
        WUKN = 18 * 512
        WUKV = 19 * 512
        RB = 20 * 512
        QT = RB
        KT = RB + 8 * 512
        VT = RB + 18 * 512
        CQN = RB
        CKVN = RB + 4 * 512
        KRT = RB + 7 * 512
        QH = [RB + 10 * 512, RB + 12 * 512]
        KH = [RB + 14 * 512, RB + 28 * 512]
        YCT = RB
        ZT = RB + 8 * 512

        def XSK(i):
            return "xs%d" % i

        def mixer(l, gi):
            T0 = gi * GT
            m = gi
            CTX = PAST if gi == 1 else 0
            NK = GT + CTX
            ring = {"n": 0, "loaded": 0, "list": []}

            def ring_view(i):
                s = i % NRING
                v, k = A(s * 1024, 1024)
                return v.rearrange("p (a n) -> p a n", n=128), k, s

            def add_chunk(loader):
                ring["list"].append(loader)

            LA = 3

            ring["held"] = 0

            def nxt():
                i = ring["n"]
                while ring["loaded"] < min(len(ring["list"]), i + LA + 1, ring["held"] + NRING):
                    j = ring["loaded"]
                    v3, k, s = ring_view(j)
                    ring["list"][j](v3, k, "wr%d" % s)
                    ring["loaded"] += 1
                assert ring["loaded"] > i, "ring overflow: too many chunks held"
                ring["n"] += 1
                v3, k, s = ring_view(i)
                return v3, k

            def release():
                ring["held"] = ring["n"]

            def simple(src_fn, a, bcols):
                def loader(v3, k, sem):
                    P.add("pool", lambda e: e.dma_start(out=v3[:, 0:a, 0:bcols], in_=src_fn()), writes=k, dma_sem=sem)
                return loader

            def win(c0, mcols):
                return simple(lambda: w_in_d[l, :, c0:c0 + mcols].rearrange("(kc p) n -> p kc n", p=128), 8, mcols)

            def krp_loader(v3, k, sem):
                P.add("pool", lambda e: e.dma_start(out=v3[:, :, 0:64],
                                                    in_=w_in_d[l, :, 3392:3456].rearrange("(kc p) n -> p kc n", p=128)),
                      writes=k, dma_sem=sem)
                for blk in range(2):
                    o = 64 + blk * 16
                    c0 = 3456 + blk * 16
                    P.add("pool", lambda e, o=o, c0=c0: e.dma_start(
                        out=v3[:, :, o:o + 8], in_=w_in_d[l, :, c0 + 8:c0 + 16].rearrange("(kc p) n -> p kc n", p=128)),
                        writes=k, dma_sem=sem)
                    P.add("pool", lambda e, o=o, c0=c0: e.dma_start(
                        out=v3[:, :, o + 8:o + 16], in_=w_in_d[l, :, c0:c0 + 8].rearrange("(kc p) n -> p kc n", p=128)),
                        writes=k, dma_sem=sem)
                    P.add("dve", lambda e, o=o: e.tensor_scalar(out=v3[:, :, o:o + 8], in0=v3[:, :, o:o + 8], scalar1=-1.0,
                                                                scalar2=None, op0=ALU.mult), reads=k, writes=k)

            for c in range(4):
                add_chunk(win(1536 + c * 128, 128))
            for c in range(4):
                add_chunk(win(2048 + c * 128, 128))
            for c in range(4):
                add_chunk(win(2560 + c * 128, 128))
            if gi == 0:
                for c in range(4):
                    add_chunk(win(2048 + c * 128, 128))
                add_chunk(win(3328, 128))
                add_chunk(win(3456, 32))
            for c in range(2):
                add_chunk(win(3072 + c * 128, 128))
            add_chunk(win(3328, 128))
            add_chunk(win(3392, 96))
            if gi == 1:
                add_chunk(krp_loader)
            for j in range(4):
                add_chunk(win(512 + j * 128, 128))
                add_chunk(win(1024 + j * 128, 128))
                add_chunk(win(j * 128, 128))
            for dc in range(8):
                for i, wd_ in enumerate((wco_d, wno_d, wmo_d)):
                    add_chunk(win(3488 + i * 1024 + dc * 128, 128))
                    add_chunk(simple((lambda wd_=wd_, dc=dc: wd_[l, :, dc * 128:(dc + 1) * 128].rearrange("(kc p) n -> p kc n", p=128)), 4, 128))
            for dc in range(8):
                add_chunk(simple((lambda dc=dc: wo_d[l, :, dc * 128:(dc + 1) * 128].rearrange("(kc p) n -> p kc n", p=128)), 8, 128))

            wuq_v, wuq_k = A(WUQ, 1536)
            wuq3 = wuq_v.rearrange("p (kc n) -> p kc n", n=768)
            P.add("pool", lambda e: e.dma_start(out=wuq3, in_=w_uq_d[l].rearrange("(kc p) n -> p kc n", p=128)),
                  writes=wuq_k, dma_sem="wq")
            wun_v, wun_k = A(WUKN, 512)
            wuv_v, wuv_k = A(WUKV, 512)
            ukv4 = w_ukv_d[l].rearrange("k (h two d) -> k h two d", two=2, d=64)
            P.add("pool", lambda e: e.dma_start(out=wun_v.rearrange("p (h d) -> p h d", d=64), in_=ukv4[:, :, 0, :]),
                  writes=wun_k, dma_sem="wq")
            P.add("pool", lambda e: e.dma_start(out=wuv_v.rearrange("p (h d) -> p h d", d=64), in_=ukv4[:, :, 1, :]),
                  writes=wuv_k, dma_sem="wq")
            wuqp_v, wuqp_k = A(WUQP, 1536)
            wuqp3 = wuqp_v.rearrange("p (kc n) -> p kc n", n=768)
            if gi == 1:
                cp("dve", wuqp_v, wuq_v, wuq_k, wuqp_k)
                for kc in range(2):
                    src = wuq3[:, kc, :].rearrange("p (h n) -> p h n", n=96)
                    dst = wuqp3[:, kc, :].rearrange("p (h n) -> p h n", n=96)
                    for blk in range(2):
                        o = 64 + blk * 16
                        P.add("dve", lambda e, src=src, dst=dst, o=o: e.tensor_scalar(
                            out=dst[:, :, o:o + 8], in0=src[:, :, o + 8:o + 16], scalar1=-1.0, scalar2=None, op0=ALU.mult),
                            reads=wuq_k, writes=wuqp_k)
                        cp("dve", dst[:, :, o + 8:o + 16], src[:, :, o:o + 8], wuq_k, wuqp_k)

            def proj_fm(M, consumer):
                w3, wk = nxt()
                for t in range(2):
                    b = bank()
                    gt = gi * 2 + t
                    for kc in range(8):
                        mm(ps[0:M, b, :], w3[:, kc, 0:M], hT[:, kc, T0 + t * TT:T0 + (t + 1) * TT],
                           kc == 0, kc == 7, wk + [HK(kc, gt)], [PSK(b)])
                    consumer(b, t)
                release()

            def proj_tm(nch, cols, consumer):
                chunks = [nxt() for _ in range(nch)]
                for q in range(8):
                    b = bank()
                    gt = gi * 2 + q // 4
                    for ci, (w3, wk) in enumerate(chunks):
                        cw = cols[ci]
                        off = sum(cols[:ci])
                        for kc in range(8):
                            mm(ps[:, b, off:off + cw], hT[:, kc, T0 + q * 128:T0 + (q + 1) * 128], w3[:, kc, 0:cw],
                               kc == 0, kc == 7, wk + [HK(kc, gt)], [PSK(b)])
                    consumer(b, q)
                release()

            mark('NA projections')
            def q_cons(c):
                def f(b, t):
                    v, k = A(QT + c * GT + t * TT, TT)
                    cp("act", v, ps[:, b, :], [PSK(b)], k)
                return f

            def k_cons(c):
                def f(b, t):
                    v, k = A(KT + c * 1280 + CTX + t * TT, TT)
                    cp("dve", v, ps[:, b, :], [PSK(b)], k)
                return f

            for c in range(4):
                proj_fm(128, q_cons(c))
            for c in range(4):
                proj_fm(128, k_cons(c))

            def out_tokmajor(dst4, b):
                sv, sk, ssem = stgslot(ext=True)
                cp("dve", sv, ps[:, b, :], [PSK(b)], [sk])
                dma("sp", dst4.rearrange("h s d -> s h d"), sv.rearrange("p (h d) -> p h d", d=64), [sk], [], ssem)

            def v_cons(b, q):
                v, k = A(VT + (CTX // 128 + q) * 512, 512)
                if gi == 0:
                    seq, half = q // 2, q % 2
                    sv, sk, ssem = stgslot(ext=True)
                    cp("dve", sv, ps[:, b, :], [PSK(b)], [sk])
                    cp("act", v, sv, [sk], k)
                    dma("sp", nv_d[seq, l, :, half * 128:(half + 1) * 128, :].rearrange("h s d -> s h d"),
                        sv.rearrange("p (h d) -> p h d", d=64), [sk], [], ssem)
                else:
                    cp("act", v, ps[:, b, :], [PSK(b)], k)

            proj_tm(4, [128] * 4, v_cons)

            if gi == 0:
                def ko_cons(b, q):
                    seq, half = q // 2, q % 2
                    out_tokmajor(nk_d[seq, l, :, half * 128:(half + 1) * 128, :], b)
                proj_tm(4, [128] * 4, ko_cons)

                def ckv_cons(b, q):
                    sv, sk, ssem = stgslot(ext=True)
                    cp("dve", sv[:, 0:160], ps[:, b, 0:160], [PSK(b)], [sk])
                    ssq, ssqk = smslot()
                    junk, jk = scr()
                    P.add("dve", lambda e, ssq=ssq: e.memset(ssq, 0.0), writes=[ssqk])
                    actf(junk[:, 0:128], sv[:, 0:128], AF.Square, [sk, ssqk], [jk, ssqk], accum_out=ssq)
                    actf(ssq, ssq, AF.Sqrt, [ssqk, "epsb"], [ssqk], scale=1.0 / 128, bias=epsb[:])
                    recip(ssq, ssq, [ssqk], [ssqk])
                    stt("dve", sv[:, 0:128], sv[:, 0:128], ssq, kvb[:, l * 128:(l + 1) * 128], ALU.mult, ALU.mult,
                        [sk, ssqk, "kvb"], [sk])
                    seq, half = q // 2, q % 2
                    dma("sp", nckv_d[seq, l, half * 128:(half + 1) * 128, :], sv[:, 0:128], [sk], [], ssem)
                    dma("sp", nkr_d[seq, l, half * 128:(half + 1) * 128, :], sv[:, 128:160], [sk], [], ssem)
                proj_tm(2, [128, 32], ckv_cons)

            mark('NA context keys / values (sample)')
            if gi == 1:
                for kc in range(2):
                    dma("sp", xs[:, kc, 0:512].rearrange("p (h d) -> p h d", d=64),
                        cnk_d[l, :, kc * 128:(kc + 1) * 128, :].rearrange("h s d -> s h d"), [], [XSK(kc)], "ld_x%d" % kc)
                for kc in range(2):
                    for c in range(4):
                        b = bank()
                        tr(ps[:, b, 0:128], xs[:, kc, c * 128:(c + 1) * 128], [XSK(kc)], [PSK(b)])
                        v, k = A(KT + c * 1280 + kc * 128, 128)
                        cp("dve" if c % 2 else "act", v, ps[:, b, 0:128], [PSK(b)], k)
                for kc in range(2):
                    vv, vk = A(VT + kc * 512, 512)
                    P.add("pool", lambda e, kc=kc, vv=vv: e.dma_start(
                        out=vv.rearrange("p (h d) -> p h d", d=64),
                        in_=cnv_d[l, :, kc * 128:(kc + 1) * 128, :].rearrange("h s d -> s h d")),
                        writes=vk, dma_sem="wq2")

            mark('attention core')
            apend = []
            LOOK = 3

            def attn_pop():
                u, hh, ptv, ptk, pa, lo, hi, vap, vk = apend.pop(0)
                hp = slice(hh * 64, hh * 64 + 64)
                n = hi - lo
                first = not u["started"][hh]
                u["started"][hh] = True
                bn, nc0 = u["num"]
                bd, dc0 = u["den"]
                mm(ps[hp, bn, nc0 + lo:nc0 + hi], vap, ptv[pa, 0:n], first, False, vk + [ptk], [PSK(bn)])
                mm(ps[hp, bd, dc0 + lo:dc0 + hi], ones[pa, 0:64], ptv[pa, 0:n], first and (bd != bn), False,
                   ["ones", ptk], [PSK(bd)])
                u["left"] -= 1
                if u["left"] == 0:
                    N = u["N"]
                    rc, rck = scr()
                    recip(rc[:, 0:N], ps[:, bd, dc0:dc0 + N], [PSK(bd)], [rck])
                    ov, ok = u["out"]
                    tt("dve", ov, ps[:, bn, nc0:nc0 + N], rc[:, 0:N], ALU.mult, [PSK(bn), rck], ok)
                    reserved.discard(bn)
                    reserved.discard(bd)

            def attn_pair(q_of, N, blocks_of, scale, out_view):
                bn = bank()
                reserved.add(bn)
                if N <= 256:
                    u = dict(num=(bn, 0), den=(bn, 256))
                else:
                    bd = bank()
                    reserved.add(bd)
                    u = dict(num=(bn, 0), den=(bd, 0))
                blist = [(hh, blk) for hh in range(2) for blk in blocks_of(hh)]
                u.update(N=N, out=out_view, left=len(blist), started=[False, False])
                qaps = [q_of(0), q_of(1)]
                for hh, blk in blist:
                    qap, qk = qaps[hh]
                    kap, kk = blk["k"]
                    pa = slice(blk["pa"][0], blk["pa"][1])
                    lo, hi = blk["qs"]
                    n = hi - lo
                    bs = bank()
                    mm(ps[pa, bs, 0:n], kap, qap[:, lo:hi], True, True, kk + qk, [PSK(bs)])
                    ptv, ptk = ptslot()
                    if blk.get("bias") is not None:
                        bap, bk = blk["bias"]
                        tmp, tk = scr()
                        stt("dve", tmp[pa, 0:n], ps[pa, bs, 0:n], scale, bap, ALU.mult, ALU.add, [PSK(bs)] + bk, [tk])
                        actf(ptv[pa, 0:n], tmp[pa, 0:n], AF.Exp, [tk], [ptk])
                    else:
                        actf(ptv[pa, 0:n], ps[pa, bs, 0:n], AF.Exp, [PSK(bs)], [ptk], scale=scale)
                    if blk.get("zero") is not None:
                        z0, z1, c0, c1 = blk["zero"]
                        P.add("dve", lambda e, ptv=ptv, z0=z0, z1=z1, c0=c0, c1=c1: e.memset(ptv[z0:z1, c0:c1], 0.0),
                              reads=[ptk], writes=[ptk])
                    vap, vk = blk["v"]
                    apend.append((u, hh, ptv, ptk, pa, lo, hi, vap, vk))
                    while len(apend) > LOOK:
                        attn_pop()

            def attn_drain():
                while apend:
                    attn_pop()

            def qna(c, lo, n):
                def f(hh):
                    v, k = A(QT + c * GT + lo, n)
                    return v[hh * 64:hh * 64 + 64, :], k
                return f

            def kna(c, hh, lo, n):
                v, k = A(KT + c * 1280 + lo, n)
                return v[hh * 64:hh * 64 + 64, :], k

            def vtile(base, h, chunk, pa):
                v, k = A(base + chunk * 512 + h * 64, 64)
                return v[pa[0]:pa[1], :], k

            if gi == 0:
                for c in range(4):
                    for s in range(4):
                        def blocks_of(hh, c=c, s=s):
                            return [dict(k=kna(c, hh, s * 256 + kb * 128, 128), pa=(0, 128),
                                         v=vtile(VT, 2 * c + hh, s * 2 + kb, (0, 128)), qs=(0, 256), bias=None)
                                    for kb in range(2)]
                        attn_pair(qna(c, s * 256, 256), 256, blocks_of, NA_SCALE,
                                  (oTn[:, c, s * 256:(s + 1) * 256], [("oTn", c, s // 2)]))
            else:
                gbuf = [bufA, bufB]
                for c in range(4):
                    for hh in range(2):
                        dma("sp", gbuf[hh][0:64, 0:960], gm_d[l, 2 * c + hh], [], GBK(hh), "ld_g%d" % hh)
                        dma("sp", gbuf[hh][64:128, 64:1024], gm_d[l, 2 * c + hh], [], GBK(hh), "ld_g%d" % hh)
                    for qt in range(2):
                        def blocks_of(hh, c=c, qt=qt):
                            bl = [dict(k=kna(c, hh, kb * 128, 128), pa=(0, 128), v=vtile(VT, 2 * c + hh, kb, (0, 128)),
                                       qs=(0, 512), bias=None) for kb in range(2)]
                            for j in range(8):
                                if j < 4:
                                    ulo, uhi = 0, 2 * j + 5
                                    rq, zp = 2 * j + 5, (0, 64)
                                else:
                                    ulo, uhi = 2 * j - 3, 15
                                    rq, zp = 2 * j - 3, (64, 128)
                                r0, r1 = max(ulo, qt * 8), min(uhi, qt * 8 + 7)
                                if r0 > r1:
                                    continue
                                i0 = 7 + r0 - 2 * j
                                n = (r1 - r0 + 1) * 64
                                zero = None
                                if r0 <= rq <= r1:
                                    zero = (zp[0], zp[1], (rq - r0) * 64, (rq - r0 + 1) * 64)
                                bl.append(dict(k=kna(c, hh, CTX + j * 128, 128), pa=(0, 128),
                                               v=vtile(VT, 2 * c + hh, 2 + j, (0, 128)),
                                               qs=((r0 - qt * 8) * 64, (r1 - qt * 8 + 1) * 64),
                                               bias=(gbuf[hh][:, i0 * 64:i0 * 64 + n], GBK(hh)), zero=zero))
                            return bl
                        attn_pair(qna(c, qt * TT, TT), TT, blocks_of, NA_SCALE,
                                  (oTn[:, c, qt * TT:(qt + 1) * TT], [("oTn", c, qt)]))

            attn_drain()
            mark('MLA')
            if gi == 1:
                dma("sp", bufA[:, :], cos_d, [], GBK(0), "ld_g0")
                dma("sp", bufB[:, :], sin_d, [], GBK(1), "ld_g1")
            def cq_cons(c):
                def f(b, t):
                    cp("act", xs[:, c, t * TT:(t + 1) * TT], ps[:, b, :], [PSK(b)], [XSK(c)])
                return f
            for c in range(2):
                proj_fm(128, cq_cons(c))
            for t in range(2):
                b = bank()
                for c in range(2):
                    sq, sqk = ptslot()
                    actf(sq, xs[:, c, t * TT:(t + 1) * TT], AF.Square, [XSK(c)], [sqk])
                    mm(ps[:, b, :], ones[:], sq, c == 0, c == 1, [sqk, "ones"], [PSK(b)])
                rs, rsk = scr()
                actf(rs, ps[:, b, :], AF.Sqrt, [PSK(b), "epsb"], [rsk], scale=1.0 / 256, bias=epsb[:])
                recip(rs, rs, [rsk], [rsk])
                for c in range(2):
                    v, k = A(CQN + c * GT + t * TT, TT)
                    stt("dve", v, xs[:, c, t * TT:(t + 1) * TT], spt[:, SP_QN + l * 2 + c:SP_QN + l * 2 + c + 1], rs,
                        ALU.mult, ALU.mult, [XSK(c), rsk, "spt"], k)

            def ckv_fm(b, t):
                ck, ckk = scr()
                cp("act", ck, ps[:, b, :], [PSK(b)], [ckk])
                sq, sqk = ptslot()
                actf(sq, ps[:, b, :], AF.Square, [PSK(b)], [sqk])
                b2 = bank()
                mm(ps[:, b2, :], ones[:], sq, True, True, [sqk, "ones"], [PSK(b2)])
                rs, rsk = scr()
                actf(rs, ps[:, b2, :], AF.Sqrt, [PSK(b2), "epsb"], [rsk], scale=1.0 / 128, bias=epsb[:])
                recip(rs, rs, [rsk], [rsk])
                v, k = A(CKVN + CTX + t * TT, TT)
                stt("dve", v, ck, spt[:, SP_KVN + l:SP_KVN + l + 1], rs, ALU.mult, ALU.mult, [ckk, rsk, "spt"], k)
            proj_fm(128, ckv_fm)

            if gi == 0:
                def kr_cons(b, t):
                    v, k = A(KRT + t * TT, TT)
                    cp("act", v[64:96, :], ps[64:96, b, :], [PSK(b)], k)
                proj_fm(96, kr_cons)
            else:
                kr_banks = {}

                def kr_cons1(b, t):
                    kr_banks[t] = b
                w3a, wka = nxt()
                w3b, wkb = nxt()
                for t in range(2):
                    ba, bb = bank(), bank()
                    for (w3, wk, bx) in ((w3a, wka, ba), (w3b, wkb, bb)):
                        for kc in range(8):
                            mm(ps[0:96, bx, :], w3[:, kc, 0:96], hT[:, kc, T0 + t * TT:T0 + (t + 1) * TT],
                               kc == 0, kc == 7, wk + [HK(kc, gi * 2 + t)], [PSK(bx)])
                    t1, t1k = scr()
                    t2, t2k = scr()
                    tt("dve", t1[64:96, :], ps[64:96, ba, :], bufA[64:96, t * TT:(t + 1) * TT], ALU.mult, [PSK(ba)] + GBK(0), [t1k])
                    tt("dve", t2[64:96, :], ps[64:96, bb, :], bufB[64:96, t * TT:(t + 1) * TT], ALU.mult, [PSK(bb)] + GBK(1), [t2k])
                    v, k = A(KRT + CTX + t * TT, TT)
                    tt("dve", v[64:96, :], t1[64:96, :], t2[64:96, :], ALU.add, [t1k, t2k], k)
                release()
                for kc in range(2):
                    sv, sk, ssem = stgslot()
                    dma("sp", sv[:, 0:128], cck_d[l, kc * 128:(kc + 1) * 128, :], [], [sk], ssem)
                    b = bank()
                    tr(ps[:, b, 0:128], sv[:, 0:128], [sk], [PSK(b)])
                    v, k = A(CKVN + kc * 128, 128)
                    cp("act", v, ps[:, b, 0:128], [PSK(b)], k)
                    sv, sk, ssem = stgslot()
                    dma("sp", sv[:, 64:96], ckr_d[l, kc * 128:(kc + 1) * 128, :], [], [sk], ssem)
                    b = bank()
                    tr(ps[0:96, b, 0:128], sv[:, 0:96], [sk], [PSK(b)])
                    v, k = A(KRT + kc * 128, 128)
                    cp("act", v[64:96, :], ps[64:96, b, 0:128], [PSK(b)], k)

            mark('v_m token-major for all heads')
            for kq in range(NK // 128):
                b = bank()
                cv, ck_ = A(CKVN + kq * 128, 128)
                mm(ps[:, b, :], cv, wuv_v, True, True, ck_ + wuv_k, [PSK(b)])
                v, k = A(VT + kq * 512, 512)
                cp("act" if kq % 2 else "dve", v, ps[:, b, :], [PSK(b)], k)

            def build_head(h):
                hh = h % 2
                for t in range(2):
                    bq = bank()
                    for kc in range(2):
                        cv, ck_ = A(CQN + kc * GT + t * TT, TT)
                        mm(ps[0:96, bq, :], wuq3[:, kc, h * 96:(h + 1) * 96], cv, kc == 0, kc == 1, wuq_k + ck_, [PSK(bq)])
                    qv, qk = A(QH[hh] + t * TT, TT)
                    if gi == 0:
                        cp("act", qv[0:96, :], ps[0:96, bq, :], [PSK(bq)], qk)
                    else:
                        bp = bank()
                        for kc in range(2):
                            cv, ck_ = A(CQN + kc * GT + t * TT, TT)
                            mm(ps[0:96, bp, :], wuqp3[:, kc, h * 96:(h + 1) * 96], cv, kc == 0, kc == 1, wuqp_k + ck_, [PSK(bp)])
                        cp("act", qv[0:64, :], ps[0:64, bq, :], [PSK(bq)], qk)
                        t1, t1k = scr()
                        t2, t2k = scr()
                        tt("dve", t1[64:96, :], ps[64:96, bq, :], bufA[64:96, t * TT:(t + 1) * TT], ALU.mult, [PSK(bq)] + GBK(0), [t1k])
                        tt("dve", t2[64:96, :], ps[64:96, bp, :], bufB[64:96, t * TT:(t + 1) * TT], ALU.mult, [PSK(bp)] + GBK(1), [t2k])
                        tt("dve", qv[64:96, :], t1[64:96, :], t2[64:96, :], ALU.add, [t1k, t2k], qk)
                lo = 0
                while lo < NK:
                    n = min(512, NK - lo)
                    b = bank()
                    cv, ck_ = A(CKVN + lo, n)
                    mm(ps[0:64, b, 0:n], wun_v[:, h * 64:(h + 1) * 64], cv, True, True, wun_k + ck_, [PSK(b)])
                    kv, kk = A(KH[hh] + lo, n)
                    cp("dve", kv[0:64, :], ps[0:64, b, 0:n], [PSK(b)], kk)
                    lo += n
                krv, krk = A(KRT, NK)
                kv, kk = A(KH[hh], NK)
                cp("act", kv[64:96, :], krv[64:96, :], krk, kk)

            for c in range(4):
                build_head(2 * c)
                build_head(2 * c + 1)

                def q_of(hh, lo, n):
                    v, k = A(QH[hh] + lo, n)
                    return v[0:96, :], k

                def kmla(hh, lo, n):
                    v, k = A(KH[hh] + lo, n)
                    return v[0:96, :], k

                if gi == 0:
                    for s in range(4):
                        def blocks_of(hh, c=c, s=s):
                            return [dict(k=kmla(hh, s * 256 + kb * 128, 128), pa=(0, 128),
                                         v=vtile(VT, 2 * c + hh, s * 2 + kb, (0, 128)), qs=(0, 256), bias=None)
                                    for kb in range(2)]
                        attn_pair(lambda hh, s=s: q_of(hh, s * 256, 256), 256, blocks_of, MLA_SCALE,
                                  (oTm[:, c, s * 256:(s + 1) * 256], [("oTm", c, s // 2)]))
                else:
                    for qt in range(2):
                        def blocks_of(hh, c=c):
                            return [dict(k=kmla(hh, kb * 128, 128), pa=(0, 128), v=vtile(VT, 2 * c + hh, kb, (0, 128)),
                                         qs=(0, 512), bias=None) for kb in range(NK // 128)]
                        attn_pair(lambda hh, qt=qt: q_of(hh, qt * TT, TT), TT, blocks_of, MLA_SCALE,
                                  (oTm[:, c, qt * TT:(qt + 1) * TT], [("oTm", c, qt)]))

            attn_drain()
            mark('short gated conv')
            nseg = 4 if gi == 0 else 1
            seglen = GT // nseg
            for j in range(4):
                cgs = []

                def cg_cons(b, t):
                    sv, sk = scr()
                    cp("act", sv, ps[:, b, :], [PSK(b)], [sk])
                    cgs.append((sv, sk))
                proj_fm(128, cg_cons)

                def xc_cons(b, t):
                    sv, sk = cgs[t]
                    tt("dve", xs[:, 0, t * TT:(t + 1) * TT], ps[:, b, :], sv, ALU.mult, [PSK(b), sk], [XSK(0)])
                proj_fm(128, xc_cons)
                cw = SP_CW + (l * 3) * 4 + j
                actf(xs[:, 1, :], xs[:, 0, :], AF.Identity, [XSK(0), "spt"], [XSK(1)],
                     scale=spt[:, cw + 4:cw + 5], bias=spt[:, SP_CB + l * 4 + j:SP_CB + l * 4 + j + 1])
                cx3 = xs[:, 0, :].rearrange("p (s n) -> p s n", n=seglen)
                t13 = xs[:, 1, :].rearrange("p (s n) -> p s n", n=seglen)
                stt("dve", t13[:, :, 1:seglen], cx3[:, :, 0:seglen - 1], spt[:, cw:cw + 1], t13[:, :, 1:seglen],
                    ALU.mult, ALU.add, [XSK(0), XSK(1), "spt"], [XSK(1)])
                stt("dve", t13[:, :, 0:seglen - 1], cx3[:, :, 1:seglen], spt[:, cw + 8:cw + 9], t13[:, :, 0:seglen - 1],
                    ALU.mult, ALU.add, [XSK(0), XSK(1), "spt"], [XSK(1)])

                def bg_cons(b, t, j=j):
                    v, k = A(YCT + j * GT + t * TT, TT)
                    tt("dve", v, ps[:, b, :], xs[:, 1, t * TT:(t + 1) * TT], ALU.mult, [PSK(b), XSK(1)], k)
                proj_fm(128, bg_cons)

            mark('gates, branch out-projections, z')
            for dc in range(8):
                for i in range(3):
                    w3, wk = nxt()
                    w3o, wko = nxt()
                    for t in range(2):
                        gt = gi * 2 + t
                        bg_ = bank()
                        for kc in range(8):
                            mm(ps[:, bg_, :], w3[:, kc, :], hT[:, kc, T0 + t * TT:T0 + (t + 1) * TT],
                               kc == 0, kc == 7, wk + [HK(kc, gt)], [PSK(bg_)])
                        bp_ = bank()
                        for kc in range(4):
                            if i == 0:
                                rv, rk = A(YCT + kc * GT + t * TT, TT)
                            elif i == 1:
                                rv, rk = oTn[:, kc, t * TT:(t + 1) * TT], [("oTn", kc, t)]
                            else:
                                rv, rk = oTm[:, kc, t * TT:(t + 1) * TT], [("oTm", kc, t)]
                            mm(ps[:, bp_, :], w3o[:, kc, :], rv, kc == 0, kc == 3, wko + rk, [PSK(bp_)])
                        tg, tgk = scr()
                        actf(tg, ps[:, bg_, :], AF.Tanh, [PSK(bg_)], [tgk], scale=0.5)
                        zacc, zak = xs[:, 0, t * TT:(t + 1) * TT], ("zacc", t)
                        if i == 0:
                            stt("dve", zacc, tg, 1.0, ps[:, bp_, :], ALU.add, ALU.mult, [tgk, PSK(bp_), XSK(0)], [zak, XSK(0)])
                        else:
                            stt("dve", tg, tg, 1.0, ps[:, bp_, :], ALU.add, ALU.mult, [tgk, PSK(bp_)], [tgk])
                            if i == 1:
                                tt("dve", zacc, zacc, tg, ALU.add, [zak, tgk, XSK(0)], [zak])
                            else:
                                zv, zk = A(ZT + dc * GT + t * TT, TT)
                                tt("dve", zv, zacc, tg, ALU.add, [zak, tgk, XSK(0)], zk)
                    release()

            mark('w_o + residual')
            for dc in range(8):
                w3, wk = nxt()
                for t in range(2):
                    gt = gi * 2 + t
                    b = bank()
                    for kc in range(8):
                        zv, zk = A(ZT + kc * GT + t * TT, TT)
                        mm(ps[:, b, :], w3[:, kc, :], zv, kc == 0, kc == 7, wk + zk, [PSK(b)])
                    xv = xT[:, dc, T0 + t * TT:T0 + (t + 1) * TT]
                    stt("dve", xv, ps[:, b, :], HGs[LC[0]][:, 1, dc, m:m + 1], xv, ALU.mult, ALU.add,
                        [PSK(b), "HG%d" % LC[0], XK(dc, gt)], [XK(dc, gt)])
                release()

        steps = []
        for l in range(L):
            steps += [("mod", l), ("norm", 0), ("ffn", l, 1), ("norm", 1), ("mix", l, 0), ("mix", l, 1),
                      ("norm", 2), ("ffn", l, 2)]
        if stage is not None:
            steps = steps[:stage]
        for si, stp in enumerate(steps):
            mark("STEP " + str(stp))
            if stp[0] == "mod":
                LC[0] = stp[1] % 2
                if stage is not None:
                    mod_compute(stp[1])
                elif stp[1] == 0:
                    mod_compute(0, nblk=6)
            elif stp[0] == "norm":
                if stp[1] == 2 and stage is None:
                    norm(2, (2, 3))
                else:
                    norm(stp[1])
            elif stp[0] == "ffn":
                extra = None
                if stp[2] == 2 and stp[1] + 1 < L and stage is None:
                    extra = mod_items_xs(stp[1] + 1)
                if stp[2] == 1 and stp[1] == 0 and stage is None:
                    extra = mod_items_xs(0, j_from=12)
                ffn(stp[1], stp[2], extra)
            else:
                mixer(stp[1], stp[2])
                if stp[2] == 0 and stage is None:
                    norm(2, (0, 1))

        mark("FINAL")
        dma("sp", bufA[:, :], fg_d, [], GBK(0), "ld_g0")
        fslots = [(xs[:, 0, :], [XSK(0)], "st_y0"), (xs[:, 1, :], [XSK(1)], "st_y1"), (bufB[:, :], GBK(1), "st_y2")]
        for q in range(16):
            t = q // 4
            sv, sk, ssem = fslots[q % 3]
            b0, b1 = bank(), bank()
            for dc in range(8):
                bb = b0 if dc < 4 else b1
                tr(ps[:, bb, (dc % 4) * 128:(dc % 4 + 1) * 128], xT[:, dc, q * 128:(q + 1) * 128], [XK(dc, t)], [PSK(bb)])
            cp("act", sv[:, 0:512], ps[:, b0, :], [PSK(b0)], sk)
            cp("dve", sv[:, 512:1024], ps[:, b1, :], [PSK(b1)], sk)
            ssq, ssqk = smslot()
            P.add("dve", lambda e, ssq=ssq: e.memset(ssq, 0.0), writes=[ssqk])
            j1, j1k = scr()
            j2, j2k = scr()
            actf(j1, sv[:, 0:512], AF.Square, sk + [ssqk], [j1k, ssqk], accum_out=ssq)
            ssq2, ssq2k = smslot()
            P.add("dve", lambda e, ssq2=ssq2: e.memset(ssq2, 0.0), writes=[ssq2k])
            actf(j2, sv[:, 512:1024], AF.Square, sk + [ssq2k], [j2k, ssq2k], accum_out=ssq2)
            tt("dve", ssq, ssq, ssq2, ALU.add, [ssqk, ssq2k], [ssqk])
            actf(ssq, ssq, AF.Sqrt, [ssqk, "epsb"], [ssqk], scale=1.0 / D, bias=epsb[:])
            recip(ssq, ssq, [ssqk], [ssqk])
            stt("dve", sv, sv, ssq, bufA[:, :], ALU.mult, ALU.mult, sk + [ssqk] + GBK(0), sk)
            dma("sp", y_d[q * 128:(q + 1) * 128, :], sv, sk, [], ssem)

        if stage is not None:
            dbg_h = dout("dbg_h", [128, 8, NTOK], BF16)
            dbg_m = dout("dbg_m", [128, 144], F32)
            dbg_a = dout("dbg_a", [128, 48], F32)
            dbg_g = dout("dbg_g", [128, 48], F32)
            dma("sp", dbg_h, hT[:], [HK(kc, t) for kc in range(8) for t in range(4)], [], "st_dbg")
            dma("sp", dbg_m, modTs[0][:].rearrange("p j m -> p (j m)"), [("mod", b, 0) for b in range(12)], [], "st_dbg")
            dma("sp", dbg_a, Ascs[0][:].rearrange("p a b c -> p (a b c)"), ["Asc0"], [], "st_dbg")
            dma("sp", dbg_g, HGs[0][:].rearrange("p a b c -> p (a b c)"), ["HG0"], [], "st_dbg")
        mark("END")
        import json, os
        if os.environ.get("KMARKS"):
            json.dump(marks, open(os.environ["KMARKS"], "w"))
        print("n_ops", P.n_ops, {e: len(P.ops[e]) for e in ENGS}, flush=True)
        P.emit(nc, st)
    return nc


_NC_CACHE = {}


def _host_consts():
    half, nf = 16, 8
    inv = (1.0 / (10000.0 ** (np.arange(nf, dtype=np.float32) / nf))).astype(np.float32)
    t = np.arange(GT)
    rows, cols = t // 64, t % 64
    cos_t = np.zeros((128, GT), np.float32)
    sin_t = np.zeros((128, GT), np.float32)
    for j in range(32):
        pos = rows if j < 16 else cols
        f = j % 8
        ang = pos.astype(np.float32) * inv[f]
        cos_t[64 + j] = np.cos(ang).astype(np.float32)
        sin_t[64 + j] = np.sin(ang).astype(np.float32)
    return cos_t, sin_t


def _rpb_gather(na_rpb):
    cq = np.arange(64)
    col_start = np.clip(cq - 8, 0, 48)
    ck = np.arange(64)
    rel = ck[:, None] - col_start[None, :]
    col_in = (rel >= 0) & (rel < 16)
    dc = np.clip(ck[:, None] - cq[None, :] + 15, 0, 30)
    ii = np.arange(15)
    g = na_rpb[:, :, (14 - ii)[None, :, None], dc[:, None, :]]
    g = np.where(col_in[None, None, :, None, :], g, np.float32(NEG)).astype(np.float32)
    return np.ascontiguousarray(g.reshape(L, 8, 64, 960))


def kernel(x_prompt, x_sample, cache_na_k, cache_na_v, cache_mla_ckv, cache_mla_krope, c, c_ctx,
           w_ada, b_ada, norm_g, w_ffn1_gate, w_ffn1_up, w_ffn1_down, w_ffn2_gate, w_ffn2_up, w_ffn2_down,
           w_in, conv_w, conv_b, na_rpb, mla_qnorm, w_uq, mla_kvnorm, w_ukv,
           w_conv_out, w_na_out, w_mla_out, w_o, final_g):
    f32 = lambda a: np.ascontiguousarray(np.asarray(a, dtype=np.float32))
    x_prompt, x_sample = f32(x_prompt), f32(x_sample)
    if "nc" not in _NC_CACHE:
        _NC_CACHE["nc"] = build_program()
    nc = _NC_CACHE["nc"]
    cos_t, sin_t = _host_consts()
    gm = _rpb_gather(f32(na_rpb))
    c = f32(c)
    c_ctx = f32(c_ctx)
    norm_g, b_ada = f32(norm_g), f32(b_ada)
    conv_w, conv_b = f32(conv_w), f32(conv_b)
    mla_qnorm, mla_kvnorm = f32(mla_qnorm), f32(mla_kvnorm)
    final_g_b = np.ascontiguousarray(np.broadcast_to(f32(final_g)[None, :], (128, D)))
    kvnorm_b = np.ascontiguousarray(np.broadcast_to(mla_kvnorm.reshape(1, L * 128), (128, L * 128)))
    shared = {
        "final_g_b": final_g_b, "kvnorm_b": kvnorm_b, "gm": gm, "cos_t": cos_t, "sin_t": sin_t,
        "w_ada": f32(w_ada), "w_ffn1_gate": f32(w_ffn1_gate), "w_ffn1_up": f32(w_ffn1_up), "w_ffn1_down": f32(w_ffn1_down),
        "w_ffn2_gate": f32(w_ffn2_gate), "w_ffn2_up": f32(w_ffn2_up), "w_ffn2_down": f32(w_ffn2_down),
        "w_in": f32(w_in), "w_uq": f32(w_uq), "w_ukv": f32(w_ukv), "w_conv_out": f32(w_conv_out),
        "w_na_out": f32(w_na_out), "w_mla_out": f32(w_mla_out), "w_o": f32(w_o),
    }
    in_maps = []
    for core in range(8):
        sp = np.zeros((128, SP_N), np.float32)
        cv = np.stack([c_ctx, c[core]], 0)
        sp[:, SP_C:SP_C + 16] = cv.reshape(2, 8, 128).transpose(2, 1, 0).reshape(128, 16)
        sp[:, SP_NG:SP_NG + 48] = norm_g.reshape(L, 3, 8, 128).transpose(3, 0, 1, 2).reshape(128, 48)
        sp[:, SP_BADA:SP_BADA + 144] = b_ada.reshape(L, 72, 128).transpose(2, 0, 1).reshape(128, 144)
        sp[:, SP_CW:SP_CW + 24] = conv_w.reshape(L, 3, 4, 128).transpose(3, 0, 1, 2).reshape(128, 24)
        sp[:, SP_CB:SP_CB + 8] = conv_b.reshape(L, 4, 128).transpose(2, 0, 1).reshape(128, 8)
        sp[:, SP_QN:SP_QN + 4] = mla_qnorm.reshape(L, 2, 128).transpose(2, 0, 1).reshape(128, 4)
        sp[:, SP_KVN:SP_KVN + 2] = mla_kvnorm.reshape(L, 128).transpose(1, 0)
        mp = dict(shared)
        mp["x_tok"] = np.ascontiguousarray(np.concatenate(
            [x_prompt[4 * core:4 * core + 4].reshape(GT, D), x_sample[core]], 0))
        mp["sp"] = sp
        mp["c_na_k"] = f32(cache_na_k[core])
        mp["c_na_v"] = f32(cache_na_v[core])
        mp["c_ckv"] = f32(cache_mla_ckv[core])
        mp["c_kr"] = f32(cache_mla_krope[core])
        in_maps.append(mp)
    res = run_bass_kernel_spmd(nc, in_maps, core_ids=list(range(8)))
    rs = res.results
    _NC_CACHE["last"] = rs
    y_prompt = np.concatenate([r["y"][:GT].reshape(4, 256, D) for r in rs], 0)
    y_sample = np.stack([r["y"][GT:] for r in rs], 0)
    new_k = np.concatenate([r["new_k"] for r in rs], 0)
    new_v = np.concatenate([r["new_v"] for r in rs], 0)
    new_ckv = np.concatenate([r["new_ckv"] for r in rs], 0)
    new_kr = np.concatenate([r["new_kr"] for r in rs], 0)
    return (y_prompt.astype(np.float32), y_sample.astype(np.float32), new_k.astype(np.float32),
            new_v.astype(np.float32), new_ckv.astype(np.float32), new_kr.astype(np.float32))
```

```python
import contextlib
import numpy as np
import concourse.bass as bass
import concourse.mybir as mybir
from concourse.bass_utils import run_bass_kernel_spmd

F32 = mybir.dt.float32
BF16 = mybir.dt.bfloat16
AF = mybir.ActivationFunctionType
ALU = mybir.AluOpType

D = 1024
FF = 2816
L = 2
NTOK = 2048
GT = 1024
TT = 512
PAST = 256
IN_DIM = 6560
EPS = 1e-6
NA_SCALE = 64 ** -0.5
MLA_SCALE = 96 ** -0.5
NEG = -30000.0
ENGS = ("pe", "act", "dve", "pool", "sp")

SP_C = 0
SP_NG = 16
SP_BADA = 64
SP_CW = 208
SP_CB = 232
SP_QN = 240
SP_KVN = 244
SP_N = 246


class Op:
    __slots__ = ("eng", "fn", "deps", "is_dma", "tok", "signal", "dma_need")

    def __init__(self, eng, fn, deps, is_dma):
        self.eng = eng
        self.fn = fn
        self.deps = deps
        self.is_dma = is_dma
        self.tok = None
        self.signal = False
        self.dma_need = {}


class Prog:
    def __init__(self):
        self.ops = {e: [] for e in ENGS}
        self.last_w = {}
        self.readers = {}
        self.dma_sems = {}
        self.n_ops = 0

    def add(self, eng, fn, reads=(), writes=(), dma_sem=None):
        deps = set()
        for k in reads:
            w = self.last_w.get(k)
            if w is not None:
                deps.add(w)
        for k in writes:
            w = self.last_w.get(k)
            if w is not None:
                deps.add(w)
            for r in self.readers.get(k, ()):
                deps.add(r)
        if eng == "pe":
            deps = {d for d in deps if d.eng != "pe"}
        op = Op(eng, fn, deps, dma_sem is not None)
        for d in deps:
            if d.is_dma:
                op.dma_need[d.tok[0]] = self.dma_sems[d.tok[0]]
        if dma_sem is not None:
            c = self.dma_sems.get(dma_sem, 0) + 16
            self.dma_sems[dma_sem] = c
            op.tok = (dma_sem, c)
        for k in reads:
            lst = self.readers.setdefault(k, [])
            if not op.is_dma:
                lst[:] = [r for r in lst if r.is_dma or r.eng != op.eng]
            lst.append(op)
        for k in writes:
            self.last_w[k] = op
            self.readers[k] = []
        self.ops[eng].append(op)
        self.n_ops += 1
        return op

    def emit(self, nc, st):
        for e in ENGS:
            for op in self.ops[e]:
                for d in op.deps:
                    if not d.is_dma:
                        d.signal = True
        for e in ENGS:
            c = 0
            for op in self.ops[e]:
                if op.is_dma:
                    continue
                if op.signal:
                    c += 1
                    op.tok = ("eng_" + e, c)
        sem_names = ["eng_" + e for e in ENGS] + sorted(self.dma_sems.keys())
        sems = {n: st.enter_context(nc.semaphore(n)) for n in sem_names}
        block = st.enter_context(nc.Block())
        prog = self

        def run(e, eng):
            waited = {}
            for op in prog.ops[e]:
                need = {}
                for d in op.deps:
                    if d.is_dma:
                        continue
                    s, v = d.tok
                    if v > need.get(s, 0):
                        need[s] = v
                for s, v in op.dma_need.items():
                    if v > need.get(s, 0):
                        need[s] = v
                for s, v in need.items():
                    if waited.get(s, 0) < v:
                        eng.wait_ge(sems[s], v)
                        waited[s] = v
                ins = op.fn(eng)
                if op.is_dma:
                    ins.then_inc(sems[op.tok[0]], 16)
                elif op.signal:
                    ins.then_inc(sems[op.tok[0]], 1)
            if e == "sp":
                for s, c in prog.dma_sems.items():
                    if waited.get(s, 0) < c:
                        eng.wait_ge(sems[s], c)

        @block.tensor
        def _(eng):
            run("pe", eng)

        @block.scalar
        def _(eng):
            run("act", eng)

        @block.vector
        def _(eng):
            run("dve", eng)

        @block.gpsimd
        def _(eng):
            run("pool", eng)

        @block.sync
        def _(eng):
            run("sp", eng)


def build_program(stage=None):
    nc = bass.Bass("TRN2", target_bir_lowering=False)
    P = Prog()

    def din(name, shape, dt=F32):
        return nc.dram_tensor(name, list(shape), dt, kind="ExternalInput").ap()

    def dout(name, shape, dt=F32):
        return nc.dram_tensor(name, list(shape), dt, kind="ExternalOutput").ap()

    x_d = din("x_tok", [NTOK, D])
    sp_d = din("sp", [128, SP_N])
    fg_d = din("final_g_b", [128, D])
    kvb_d = din("kvnorm_b", [128, L * 128])
    cnk_d = din("c_na_k", [L, 8, PAST, 64])
    cnv_d = din("c_na_v", [L, 8, PAST, 64])
    cck_d = din("c_ckv", [L, PAST, 128])
    ckr_d = din("c_kr", [L, PAST, 32])
    gm_d = din("gm", [L, 8, 64, 960])
    cos_d = din("cos_t", [128, GT])
    sin_d = din("sin_t", [128, GT])
    w_ada_d = din("w_ada", [L, D, 9 * D])
    wf_d = {}
    for nm, shp in (("w_ffn1_gate", [L, D, FF]), ("w_ffn1_up", [L, D, FF]), ("w_ffn1_down", [L, FF, D]),
                    ("w_ffn2_gate", [L, D, FF]), ("w_ffn2_up", [L, D, FF]), ("w_ffn2_down", [L, FF, D])):
        wf_d[nm] = din(nm, shp)
    w_in_d = din("w_in", [L, D, IN_DIM])
    w_uq_d = din("w_uq", [L, 256, 768])
    w_ukv_d = din("w_ukv", [L, 128, 1024])
    wco_d = din("w_conv_out", [L, 512, D])
    wno_d = din("w_na_out", [L, 512, D])
    wmo_d = din("w_mla_out", [L, 512, D])
    wo_d = din("w_o", [L, D, D])

    y_d = dout("y", [NTOK, D])
    nk_d = dout("new_k", [4, L, 8, 256, 64])
    nv_d = dout("new_v", [4, L, 8, 256, 64])
    nckv_d = dout("new_ckv", [4, L, 256, 128])
    nkr_d = dout("new_kr", [4, L, 256, 32])

    with contextlib.ExitStack() as st:
        def sb(name, shape, dt):
            return st.enter_context(nc.sbuf_tensor(name, list(shape), dt))

        xT = sb("xT", [128, 8, NTOK], F32)
        hT = sb("hT", [128, 8, NTOK], BF16)
        NATOM = 52
        AR = sb("arena", [128, NATOM * 512], BF16)
        oTn = sb("oTn", [128, 4, GT], BF16)
        oTm = sb("oTm", [128, 4, GT], BF16)
        NPT = 4
        ptr = sb("ptr", [128, NPT, 512], BF16)
        bufA = sb("bufA", [128, 1024], F32)
        bufB = sb("bufB", [128, 1024], F32)
        NSCR = 5
        scr_t = sb("scr", [128, NSCR, 512], F32)
        xs = sb("xs", [128, 2, 1024], F32)
        NSTG = 2
        stg = sb("stg", [128, NSTG, 512], F32)
        spt = sb("spt", [128, SP_N], F32)
        modTs = [sb("modT%d" % i, [128, 72, 2], F32) for i in range(2)]
        Ascs = [sb("Asc%d" % i, [128, 3, 8, 2], F32) for i in range(2)]
        HGs = [sb("HG%d" % i, [128, 3, 8, 2], F32) for i in range(2)]
        wada = sb("wada", [128, 2, 1024], BF16)
        LC = [0]
        csil = sb("csil", [128, 16], BF16)
        ident = sb("ident", [128, 128], F32)
        ones = sb("ones", [128, 128], BF16)
        kvb = sb("kvb", [128, L * 128], F32)
        sm = sb("sm", [128, 16], F32)
        epsb = sb("epsb", [128, 1], F32)
        ps = st.enter_context(nc.psum_tensor("ps", [128, 8, 512], F32))
        print("sbuf bytes remaining", nc.sbuf_bytes_remaining, flush=True)

        cnt = {"bank": 0, "scr": 0, "pt": 0, "stg": 0, "stgx": 0, "sm": 0, "q": 0}

        reserved = set()

        def bank():
            while True:
                b = cnt["bank"] % 8
                cnt["bank"] += 1
                if b not in reserved:
                    return b

        def scr():
            i = cnt["scr"] % NSCR
            cnt["scr"] += 1
            return scr_t[:, i, :], ("scr", i)

        def ptslot():
            i = cnt["pt"] % NPT
            cnt["pt"] += 1
            return ptr[:, i, :], ("pt", i)

        def stgslot(ext=False):
            if ext:
                i = cnt["stgx"] % (NSTG + 2)
                cnt["stgx"] += 1
                if i >= NSTG:
                    return xs[:, i - NSTG, 0:512], "xs%d" % (i - NSTG), "st_%d" % i
                return stg[:, i, :], ("stg", i), "st_%d" % i
            i = cnt["stg"] % NSTG
            cnt["stg"] += 1
            return stg[:, i, :], ("stg", i), "st_%d" % i

        def smslot():
            i = cnt["sm"] % 16
            cnt["sm"] += 1
            return sm[:, i:i + 1], ("sm", i)

        def dmaq():
            return "sp"

        def A(off, n):
            keys = [("A", a) for a in range(off // 512, (off + n - 1) // 512 + 1)]
            return AR[:, off:off + n], keys

        def PSK(b):
            return ("ps", b)

        def mm(out, lhsT, rhs, start, stop, reads, writes):
            P.add("pe", lambda e: e.matmul(out, lhsT=lhsT, rhs=rhs, start=start, stop=stop, skip_group_check=True),
                  reads=reads, writes=writes)

        def tr(out, in_, reads, writes):
            P.add("pe", lambda e: e.transpose(out=out, in_=in_, identity=ident[:]), reads=list(reads) + ["ident"], writes=writes)

        def actf(out, in_, func, reads, writes, scale=1.0, bias=None, accum_out=None):
            def f(e):
                kw = {}
                if bias is not None:
                    kw["bias"] = bias
                if accum_out is not None:
                    kw["accum_out"] = accum_out
                return e.activation(out=out, in_=in_, func=func, scale=scale, **kw)
            P.add("act", f, reads=reads, writes=writes)

        def tt(eng, out, in0, in1, op, reads, writes):
            P.add(eng, lambda e: e.tensor_tensor(out=out, in0=in0, in1=in1, op=op), reads=reads, writes=writes)

        def stt(eng, out, in0, scalar, in1, op0, op1, reads, writes):
            P.add(eng, lambda e: e.scalar_tensor_tensor(out=out, in0=in0, scalar=scalar, in1=in1, op0=op0, op1=op1),
                  reads=reads, writes=writes)

        def ts(eng, out, in0, s1, op0, reads, writes, s2=None, op1=None):
            def f(e):
                if op1 is None:
                    return e.tensor_scalar(out=out, in0=in0, scalar1=s1, scalar2=None, op0=op0)
                return e.tensor_scalar(out=out, in0=in0, scalar1=s1, scalar2=s2, op0=op0, op1=op1)
            P.add(eng, f, reads=reads, writes=writes)

        def cp(eng, out, in_, reads, writes):
            if eng == "act":
                actf(out, in_, AF.Identity, reads, writes)
            else:
                P.add(eng, lambda e: e.tensor_copy(out=out, in_=in_), reads=reads, writes=writes)

        def recip(out, in_, reads, writes):
            P.add("dve", lambda e: e.reciprocal(out=out, in_=in_), reads=reads, writes=writes)

        def dma(eng, out, in_, reads, writes, sem):
            P.add(eng, lambda e: e.dma_start(out=out, in_=in_), reads=reads, writes=writes, dma_sem=sem)

        def GBK(i):
            return [("gb", i)] + ([("gb1h", 0), ("gb1h", 1)] if i == 1 else [("gbh", 0), ("gbh", 1)])

        marks = []

        def mark(label):
            marks.append((label, len(P.ops["pe"])))

        def XK(dc, t):
            return ("x", dc, t)

        def HK(kc, t):
            return ("h", kc, t)

        P.add("pool", lambda e: e.memset(ident[:], 0.0), writes=["ident"])
        P.add("pool", lambda e: e.affine_select(out=ident[:], in_=ident[:], pattern=[[-1, 128]], compare_op=ALU.not_equal,
                                                fill=1.0, base=0, channel_multiplier=1), reads=["ident"], writes=["ident"])
        P.add("dve", lambda e: e.memset(ones[:], 1.0), writes=["ones"])
        P.add("dve", lambda e: e.memset(epsb[:], EPS), writes=["epsb"])
        P.add("dve", lambda e: e.memset(xs[:], 0.0), writes=["xs0", "xs1"])
        dma("sp", spt[:], sp_d, [], ["spt"], "ld_c")
        dma("sp", kvb[:], kvb_d, [], ["kvb"], "ld_c")
        actf(csil[:], spt[:, SP_C:SP_C + 16], AF.Silu, ["spt"], ["csil"])

        for t in range(4):
            banks = [bank() for _ in range(8)]
            for qq in range(4):
                q = t * 4 + qq
                lslots = [(xs[:, 0, :], ["xs0"], "ld_x0"), (xs[:, 1, :], ["xs1"], "ld_x1"),
                          (bufA[:, :], GBK(0), "ld_g0"), (bufB[:, :], GBK(1), "ld_g1")]
                lv, lk, lsem = lslots[q % 4]
                dma("sp", lv, x_d[q * 128:(q + 1) * 128, :], [], lk, lsem)
                for dc in range(8):
                    tr(ps[:, banks[dc], qq * 128:(qq + 1) * 128], lv[:, dc * 128:(dc + 1) * 128],
                       lk, [PSK(banks[dc])])
            for dc in range(8):
                cp("dve" if dc % 2 == 0 else "act", xT[:, dc, t * TT:(t + 1) * TT], ps[:, banks[dc], :],
                   [PSK(banks[dc])], [XK(dc, t)])

        W_OFF = [0, 12 * 512, 24 * 512]
        ACT_OFF = [36 * 512, 44 * 512]

        def mod_items(l):
            par = l % 2
            modT, Asc, HG = modTs[par], Ascs[par], HGs[par]
            items = []
            state = {"bank": None}

            def load(j):
                sl = j % 2
                w3 = wada[:, sl, :].rearrange("p (kc n) -> p kc n", n=128)
                P.add("pool", lambda e: e.dma_start(
                    out=w3, in_=w_ada_d[l, :, j * 128:(j + 1) * 128].rearrange("(kc p) n -> p kc n", p=128)),
                    writes=[("wada", sl)], dma_sem="wa%d" % sl)

            def chunk(j):
                def f():
                    if j == 0:
                        load(0)
                        load(1)
                    sl = j % 2
                    w3 = wada[:, sl, :].rearrange("p (kc n) -> p kc n", n=128)
                    jj = j % 6
                    if jj == 0:
                        state["bank"] = bank()
                        reserved.add(state["bank"])
                    b = state["bank"]
                    for kc in range(8):
                        mm(ps[:, b, jj * 2:jj * 2 + 2], w3[:, kc, :], csil[:, kc * 2:kc * 2 + 2],
                           kc == 0, kc == 7, [("wada", sl), "csil"], [PSK(b)])
                    if j + 2 < 72:
                        load(j + 2)
                    if jj == 5:
                        blk = j // 6
                        for m in range(2):
                            psv = ps[:, b, 0:12].rearrange("p (j m) -> p j m", m=2)[:, :, m]
                            tt("dve", modT[:, blk * 6:(blk + 1) * 6, m], psv,
                               spt[:, SP_BADA + l * 72 + blk * 6: SP_BADA + l * 72 + blk * 6 + 6], ALU.add,
                               [PSK(b), "spt"], [("mod", blk, par)])
                        reserved.discard(b)
                return f
            for j in range(72):
                items.append(chunk(j))

            def fin():
                allmod = [("mod", blk, par) for blk in range(12)]
                for i3 in range(3):
                    for m in range(2):
                        j0 = (3 * i3 + 1) * 8
                        stt("dve", Asc[:, i3, :, m], modT[:, j0:j0 + 8, m], 1.0,
                            spt[:, SP_NG + (l * 3 + i3) * 8: SP_NG + (l * 3 + i3) * 8 + 8], ALU.add, ALU.mult,
                            allmod + ["spt"], ["Asc%d" % par])
                        j1 = (3 * i3 + 2) * 8
                        ts("dve", HG[:, i3, :, m], modT[:, j1:j1 + 8, m], 0.5, ALU.mult, allmod, ["HG%d" % par])
            items.append(fin)
            return items

        def mod_fin(l, i3s):
            par = l % 2
            modT, Asc, HG = modTs[par], Ascs[par], HGs[par]

            def fin():
                allmod = [("mod", blk, par) for blk in range(12)]
                for i3 in i3s:
                    for m in range(2):
                        j0 = (3 * i3 + 1) * 8
                        stt("dve", Asc[:, i3, :, m], modT[:, j0:j0 + 8, m], 1.0,
                            spt[:, SP_NG + (l * 3 + i3) * 8: SP_NG + (l * 3 + i3) * 8 + 8], ALU.add, ALU.mult,
                            allmod + ["spt"], ["Asc%d" % par])
                        j1 = (3 * i3 + 2) * 8
                        ts("dve", HG[:, i3, :, m], modT[:, j1:j1 + 8, m], 0.5, ALU.mult, allmod, ["HG%d" % par])
            return fin

        def mod_block_a(l, j, w3, wkeys, W=512):
            b = bank()
            for kc in range(8):
                mm(ps[0:2, b, 0:W], csil[:, kc * 2:kc * 2 + 2], w3[:, kc, :], kc == 0, kc == 7, wkeys + ["csil"], [PSK(b)])
            rb, rbk = bufA[0:2, (j % 2) * 512:(j % 2) * 512 + W], ("gbh", j % 2)
            cp("act", rb, ps[0:2, b, 0:W], [PSK(b), ("gb", 0)], [rbk])

        def mod_block_b(l, j, W=512):
            par = l % 2
            modT = modTs[par]
            rb, rbk = bufA[0:2, (j % 2) * 512:(j % 2) * 512 + W], ("gbh", j % 2)
            nj = W // 128
            b2 = bank()
            for jj in range(nj):
                P.add("pe", lambda e, jj=jj, b2=b2, rb=rb: e.transpose(out=ps[:, b2, jj * 2:jj * 2 + 2],
                                                                       in_=rb[:, jj * 128:(jj + 1) * 128], identity=ident[0:2, 0:2]),
                      reads=[rbk, "ident"], writes=[PSK(b2)])
            for m in range(2):
                psv = ps[:, b2, 0:2 * nj].rearrange("p (j m) -> p j m", m=2)[:, :, m]
                tt("dve", modT[:, j * nj:(j + 1) * nj, m], psv,
                   spt[:, SP_BADA + l * 72 + j * nj: SP_BADA + l * 72 + (j + 1) * nj], ALU.add,
                   [PSK(b2), "spt"], [("mod", (j * nj) // 6, par), ("mod", (j * nj + nj - 1) // 6, par)])

        def mod_items_xs(l, j_from=0):
            items = []

            def wview(j):
                xb = xs[:, j % 2, :].bitcast(BF16)
                return xb.rearrange("p (kc n) -> p kc n", n=256), [XSK(j % 2)]

            def load(j):
                w3, wk = wview(j)
                P.add("pool", lambda e: e.dma_start(
                    out=w3, in_=w_ada_d[l, :, j * 256:(j + 1) * 256].rearrange("(kc p) n -> p kc n", p=128)),
                    writes=wk, dma_sem="wa%d" % (j % 2))

            def blockf(j):
                def f():
                    if j == j_from:
                        load(j)
                        if j + 1 < 36:
                            load(j + 1)
                    else:
                        mod_block_b(l, j - 1, 256)
                    w3, wk = wview(j)
                    mod_block_a(l, j, w3, wk, 256)
                    if j + 2 < 36:
                        load(j + 2)
                return f
            for j in range(j_from, 36):
                items.append(blockf(j))
            fin_ = mod_fin(l, (0, 1, 2) if j_from == 0 else (1, 2))

            def last():
                mod_block_b(l, 35, 256)
                fin_()
            items.append(last)
            return items

        def mod_compute(l, nblk=18):
            if stage is not None:
                for it in mod_items(l):
                    it()
                return
            for j in range(nblk):
                s_ = j % 3
                wv, wk = A(W_OFF[s_], 4096)
                w3 = wv.rearrange("p (kc n) -> p kc n", n=512)
                P.add("pool", lambda e, w3=w3, j=j: e.dma_start(
                    out=w3, in_=w_ada_d[l, :, j * 512:(j + 1) * 512].rearrange("(kc p) n -> p kc n", p=128)),
                    writes=wk, dma_sem="wf%d" % s_)
                mod_block_a(l, j, w3, wk)
                if j > 0:
                    mod_block_b(l, j - 1)
            mod_block_b(l, nblk - 1)
            mod_fin(l, (0, 1, 2) if nblk == 18 else (0,))()

        def norm(i3, tiles=(0, 1, 2, 3)):
            par = LC[0]
            modT, Asc = modTs[par], Ascs[par]
            nb = {}
            for t in tiles:
                b = bank()
                reserved.add(b)
                nb[t] = b
                for dc in range(8):
                    sq, sqk = ptslot()
                    if dc % 2 == 0:
                        tt("pool", sq, xT[:, dc, t * TT:(t + 1) * TT], xT[:, dc, t * TT:(t + 1) * TT], ALU.mult, [XK(dc, t)], [sqk])
                    else:
                        actf(sq, xT[:, dc, t * TT:(t + 1) * TT], AF.Square, [XK(dc, t)], [sqk])
                    mm(ps[:, b, :], ones[:], sq, dc == 0, dc == 7, [sqk, "ones"], [PSK(b)])
            for t in tiles:
                m = t // 2
                b = nb[t]
                rs, rsk = bufB[:, (t % 2) * TT:(t % 2 + 1) * TT], ("gb1h", t % 2)
                actf(rs, ps[:, b, :], AF.Sqrt, [PSK(b), "epsb"], [rsk], scale=1.0 / D, bias=epsb[:])
                reserved.discard(b)
                recip(rs, rs, [rsk], [rsk])
                for dc in range(8):
                    tmp, tk = scr()
                    tt("dve", tmp, xT[:, dc, t * TT:(t + 1) * TT], rs, ALU.mult, [XK(dc, t), rsk], [tk])
                    actf(hT[:, dc, t * TT:(t + 1) * TT], tmp, AF.Identity, [tk, "Asc%d" % par, ("mod", (3 * i3 * 8 + dc) // 6, par)], [HK(dc, t)],
                         scale=Asc[:, i3, dc, m:m + 1], bias=modT[:, 3 * i3 * 8 + dc, m:m + 1])

        def ffn(l, which, extra=None):
            par = LC[0]
            HG = HGs[par]
            HGK = "HG%d" % par
            wg_d = wf_d["w_ffn%d_gate" % which][l]
            wu_d = wf_d["w_ffn%d_up" % which][l]
            wd_d = wf_d["w_ffn%d_down" % which][l]
            i3 = 0 if which == 1 else 2
            NG = 11
            ex = list(extra) if extra else []
            per = (len(ex) + NG - 1) // NG if ex else 0
            tickn = [0]

            def views(g):
                s = g % 3
                wgv, wgk = A(W_OFF[s], 2048)
                wuv, wuk = A(W_OFF[s] + 2048, 2048)
                wdv, wdk = A(W_OFF[s] + 4096, 2048)
                return (wgv.rearrange("p (kc n) -> p kc n", n=256), wgk,
                        wuv.rearrange("p (kc n) -> p kc n", n=256), wuk,
                        wdv.rearrange("p (j n) -> p j n", n=1024), wdk, s)

            def load(g):
                wg3, wgk, wu3, wuk, wd3, wdk, s = views(g)
                f0 = g * 256
                P.add("pool", lambda e: e.dma_start(out=wg3, in_=wg_d[:, f0:f0 + 256].rearrange("(kc p) n -> p kc n", p=128)),
                      writes=wgk, dma_sem="wf%d" % s)
                P.add("pool", lambda e: e.dma_start(out=wu3, in_=wu_d[:, f0:f0 + 256].rearrange("(kc p) n -> p kc n", p=128)),
                      writes=wuk, dma_sem="wf%d" % s)
                P.add("pool", lambda e: e.dma_start(out=wd3, in_=wd_d[f0:f0 + 256, :].rearrange("(j p) n -> p j n", p=128)),
                      writes=wdk, dma_sem="wf%d" % s)

            def actview(g, j, t):
                sa = g % 2
                return A(ACT_OFF[sa] + j * NTOK + t * TT, TT)

            def gu_unit(g, j, t):
                wg3, wgk, wu3, wuk, wd3, wdk, s = views(g)
                bg, bu = bank(), bank()
                for kc in range(8):
                    mm(ps[:, bg, :], wg3[:, kc, j * 128:(j + 1) * 128], hT[:, kc, t * TT:(t + 1) * TT],
                       kc == 0, kc == 7, wgk + [HK(kc, t)], [PSK(bg)])
                for kc in range(8):
                    mm(ps[:, bu, :], wu3[:, kc, j * 128:(j + 1) * 128], hT[:, kc, t * TT:(t + 1) * TT],
                       kc == 0, kc == 7, wuk + [HK(kc, t)], [PSK(bu)])
                sg, sgk = scr()
                actf(sg, ps[:, bg, :], AF.Silu, [PSK(bg)], [sgk])
                av, ak = actview(g, j, t)
                tt("dve", av, ps[:, bu, :], sg, ALU.mult, [PSK(bu), sgk], ak)

            def down_unit(g, t, dc):
                wg3, wgk, wu3, wuk, wd3, wdk, s = views(g)
                m = t // 2
                b = bank()
                for j in range(2):
                    av, ak = actview(g, j, t)
                    mm(ps[:, b, :], wd3[:, j, dc * 128:(dc + 1) * 128], av, j == 0, j == 1, wdk + ak, [PSK(b)])
                xv = xT[:, dc, t * TT:(t + 1) * TT]
                if dc % 2 == 0:
                    stt("dve", xv, ps[:, b, :], HG[:, i3, dc, m:m + 1], xv, ALU.mult, ALU.add,
                        [PSK(b), HGK, XK(dc, t)], [XK(dc, t)])
                else:
                    tmp, tk = scr()
                    actf(tmp, ps[:, b, :], AF.Identity, [PSK(b), HGK], [tk], scale=HG[:, i3, dc, m:m + 1])
                    tt("pool", xv, xv, tmp, ALU.add, [tk, XK(dc, t)], [XK(dc, t)])

            load(0)
            ucount = [0]
            every = max(1, (NG * 8 - 4) // len(ex)) if ex else 0
            for g in range(NG):
                if g + 1 < NG:
                    load(g + 1)
                dunits = [(t, dc) for t in range(4) for dc in range(8)] if g >= 1 else []
                for j in range(2):
                    for t in range(4):
                        gu_unit(g, j, t)
                        ucount[0] += 1
                        if ex and ucount[0] % every == 0:
                            ex.pop(0)()
                        for _ in range(4):
                            if dunits:
                                down_unit(g - 1, *dunits.pop(0))
            for t in range(4):
                for dc in range(8):
                    down_unit(NG - 1, t, dc)
            while ex:
                ex.pop(0)()

        NRING = 6
        WUQ = 12 * 512
        WUQP = 15 * 512
        WUKN = 18 * 512
        WUKV = 19 * 512
        RB = 20 * 512
        QT = RB
        KT = RB + 8 * 512
        VT = RB + 18 * 512
        CQN = RB
        CKVN = RB + 4 * 512
        KRT = RB + 7 * 512
        QH = [RB + 10 * 512, RB + 12 * 512]
        KH = [RB + 14 * 512, RB + 28 * 512]
        YCT = RB
        ZT = RB + 8 * 512

        def XSK(i):
            return "xs%d" % i

        def mixer(l, gi):
            T0 = gi * GT
            m = gi
            CTX = PAST if gi == 1 else 0
            NK = GT + CTX
            ring = {"n": 0, "loaded": 0, "list": []}

            def ring_view(i):
                s = i % NRING
                v, k = A(s * 1024, 1024)
                return v.rearrange("p (a n) -> p a n", n=128), k, s

            def add_chunk(loader):
                ring["list"].append(loader)

            LA = 3

            ring["held"] = 0

            def nxt():
                i = ring["n"]
                while ring["loaded"] < min(len(ring["list"]), i + LA + 1, ring["held"] + NRING):
                    j = ring["loaded"]
                    v3, k, s = ring_view(j)
                    ring["list"][j](v3, k, "wr%d" % s)
                    ring["loaded"] += 1
                assert ring["loaded"] > i, "ring overflow: too many chunks held"
                ring["n"] += 1
                v3, k, s = ring_view(i)
                return v3, k

            def release():
                ring["held"] = ring["n"]

            def simple(src_fn, a, bcols):
                def loader(v3, k, sem):
                    P.add("pool", lambda e: e.dma_start(out=v3[:, 0:a, 0:bcols], in_=src_fn()), writes=k, dma_sem=sem)
                return loader

            def win(c0, mcols):
                return simple(lambda: w_in_d[l, :, c0:c0 + mcols].rearrange("(kc p) n -> p kc n", p=128), 8, mcols)

            def krp_loader(v3, k, sem):
                P.add("pool", lambda e: e.dma_start(out=v3[:, :, 0:64],
                                                    in_=w_in_d[l, :, 3392:3456].rearrange("(kc p) n -> p kc n", p=128)),
                      writes=k, dma_sem=sem)
                for blk in range(2):
                    o = 64 + blk * 16
                    c0 = 3456 + blk * 16
                    P.add("pool", lambda e, o=o, c0=c0: e.dma_start(
                        out=v3[:, :, o:o + 8], in_=w_in_d[l, :, c0 + 8:c0 + 16].rearrange("(kc p) n -> p kc n", p=128)),
                        writes=k, dma_sem=sem)
                    P.add("pool", lambda e, o=o, c0=c0: e.dma_start(
                        out=v3[:, :, o + 8:o + 16], in_=w_in_d[l, :, c0:c0 + 8].rearrange("(kc p) n -> p kc n", p=128)),
                        writes=k, dma_sem=sem)
                    P.add("dve", lambda e, o=o: e.tensor_scalar(out=v3[:, :, o:o + 8], in0=v3[:, :, o:o + 8], scalar1=-1.0,
                                                                scalar2=None, op0=ALU.mult), reads=k, writes=k)

            for c in range(4):
                add_chunk(win(1536 + c * 128, 128))
            for c in range(4):
                add_chunk(win(2048 + c * 128, 128))
            for c in range(4):
                add_chunk(win(2560 + c * 128, 128))
            if gi == 0:
                for c in range(4):
                    add_chunk(win(2048 + c * 128, 128))
                add_chunk(win(3328, 128))
                add_chunk(win(3456, 32))
            for c in range(2):
                add_chunk(win(3072 + c * 128, 128))
            add_chunk(win(3328, 128))
            add_chunk(win(3392, 96))
            if gi == 1:
                add_chunk(krp_loader)
            for j in range(4):
                add_chunk(win(512 + j * 128, 128))
                add_chunk(win(1024 + j * 128, 128))
                add_chunk(win(j * 128, 128))
            for dc in range(8):
                for i, wd_ in enumerate((wco_d, wno_d, wmo_d)):
                    add_chunk(win(3488 + i * 1024 + dc * 128, 128))
                    add_chunk(simple((lambda wd_=wd_, dc=dc: wd_[l, :, dc * 128:(dc + 1) * 128].rearrange("(kc p) n -> p kc n", p=128)), 4, 128))
            for dc in range(8):
                add_chunk(simple((lambda dc=dc: wo_d[l, :, dc * 128:(dc + 1) * 128].rearrange("(kc p) n -> p kc n", p=128)), 8, 128))

            wuq_v, wuq_k = A(WUQ, 1536)
            wuq3 = wuq_v.rearrange("p (kc n) -> p kc n", n=768)
            P.add("pool", lambda e: e.dma_start(out=wuq3, in_=w_uq_d[l].rearrange("(kc p) n -> p kc n", p=128)),
                  writes=wuq_k, dma_sem="wq")
            wun_v, wun_k = A(WUKN, 512)
            wuv_v, wuv_k = A(WUKV, 512)
            ukv4 = w_ukv_d[l].rearrange("k (h two d) -> k h two d", two=2, d=64)
            P.add("pool", lambda e: e.dma_start(out=wun_v.rearrange("p (h d) -> p h d", d=64), in_=ukv4[:, :, 0, :]),
                  writes=wun_k, dma_sem="wq")
            P.add("pool", lambda e: e.dma_start(out=wuv_v.rearrange("p (h d) -> p h d", d=64), in_=ukv4[:, :, 1, :]),
                  writes=wuv_k, dma_sem="wq")
            wuqp_v, wuqp_k = A(WUQP, 1536)
            wuqp3 = wuqp_v.rearrange("p (kc n) -> p kc n", n=768)
            if gi == 1:
                cp("dve", wuqp_v, wuq_v, wuq_k, wuqp_k)
                for kc in range(2):
                    src = wuq3[:, kc, :].rearrange("p (h n) -> p h n", n=96)
                    dst = wuqp3[:, kc, :].rearrange("p (h n) -> p h n", n=96)
                    for blk in range(2):
                        o = 64 + blk * 16
                        P.add("dve", lambda e, src=src, dst=dst, o=o: e.tensor_scalar(
                            out=dst[:, :, o:o + 8], in0=src[:, :, o + 8:o + 16], scalar1=-1.0, scalar2=None, op0=ALU.mult),
                            reads=wuq_k, writes=wuqp_k)
                        cp("dve", dst[:, :, o + 8:o + 16], src[:, :, o:o + 8], wuq_k, wuqp_k)

            def proj_fm(M, consumer):
                w3, wk = nxt()
                for t in range(2):
                    b = bank()
                    gt = gi * 2 + t
                    for kc in range(8):
                        mm(ps[0:M, b, :], w3[:, kc, 0:M], hT[:, kc, T0 + t * TT:T0 + (t + 1) * TT],
                           kc == 0, kc == 7, wk + [HK(kc, gt)], [PSK(b)])
                    consumer(b, t)
                release()

            def proj_tm(nch, cols, consumer):
                chunks = [nxt() for _ in range(nch)]
                for q in range(8):
                    b = bank()
                    gt = gi * 2 + q // 4
                    for ci, (w3, wk) in enumerate(chunks):
                        cw = cols[ci]
                        off = sum(cols[:ci])
                        for kc in range(8):
                            mm(ps[:, b, off:off + cw], hT[:, kc, T0 + q * 128:T0 + (q + 1) * 128], w3[:, kc, 0:cw],
                               kc == 0, kc == 7, wk + [HK(kc, gt)], [PSK(b)])
                    consumer(b, q)
                release()

            mark('NA projections')
            def q_cons(c):
                def f(b, t):
                    v, k = A(QT + c * GT + t * TT, TT)
                    cp("act", v, ps[:, b, :], [PSK(b)], k)
                return f

            def k_cons(c):
                def f(b, t):
                    v, k = A(KT + c * 1280 + CTX + t * TT, TT)
                    cp("dve", v, ps[:, b, :], [PSK(b)], k)
                return f

            for c in range(4):
                proj_fm(128, q_cons(c))
            for c in range(4):
                proj_fm(128, k_cons(c))

            def out_tokmajor(dst4, b):
                sv, sk, ssem = stgslot(ext=True)
                cp("dve", sv, ps[:, b, :], [PSK(b)], [sk])
                dma("sp", dst4.rearrange("h s d -> s h d"), sv.rearrange("p (h d) -> p h d", d=64), [sk], [], ssem)

            def v_cons(b, q):
                v, k = A(VT + (CTX // 128 + q) * 512, 512)
                if gi == 0:
                    seq, half = q // 2, q % 2
                    sv, sk, ssem = stgslot(ext=True)
                    cp("dve", sv, ps[:, b, :], [PSK(b)], [sk])
                    cp("act", v, sv, [sk], k)
                    dma("sp", nv_d[seq, l, :, half * 128:(half + 1) * 128, :].rearrange("h s d -> s h d"),
                        sv.rearrange("p (h d) -> p h d", d=64), [sk], [], ssem)
                else:
                    cp("act", v, ps[:, b, :], [PSK(b)], k)

            proj_tm(4, [128] * 4, v_cons)

            if gi == 0:
                def ko_cons(b, q):
                    seq, half = q // 2, q % 2
                    out_tokmajor(nk_d[seq, l, :, half * 128:(half + 1) * 128, :], b)
                proj_tm(4, [128] * 4, ko_cons)

                def ckv_cons(b, q):
                    sv, sk, ssem = stgslot(ext=True)
                    cp("dve", sv[:, 0:160], ps[:, b, 0:160], [PSK(b)], [sk])
                    ssq, ssqk = smslot()
                    junk, jk = scr()
                    P.add("dve", lambda e, ssq=ssq: e.memset(ssq, 0.0), writes=[ssqk])
                    actf(junk[:, 0:128], sv[:, 0:128], AF.Square, [sk, ssqk], [jk, ssqk], accum_out=ssq)
                    actf(ssq, ssq, AF.Sqrt, [ssqk, "epsb"], [ssqk], scale=1.0 / 128, bias=epsb[:])
                    recip(ssq, ssq, [ssqk], [ssqk])
                    stt("dve", sv[:, 0:128], sv[:, 0:128], ssq, kvb[:, l * 128:(l + 1) * 128], ALU.mult, ALU.mult,
                        [sk, ssqk, "kvb"], [sk])
                    seq, half = q // 2, q % 2
                    dma("sp", nckv_d[seq, l, half * 128:(half + 1) * 128, :], sv[:, 0:128], [sk], [], ssem)
                    dma("sp", nkr_d[seq, l, half * 128:(half + 1) * 128, :], sv[:, 128:160], [sk], [], ssem)
                proj_tm(2, [128, 32], ckv_cons)

            mark('NA context keys / values (sample)')
            if gi == 1:
                for kc in range(2):
                    dma("sp", xs[:, kc, 0:512].rearrange("p (h d) -> p h d", d=64),
                        cnk_d[l, :, kc * 128:(kc + 1) * 128, :].rearrange("h s d -> s h d"), [], [XSK(kc)], "ld_x%d" % kc)
                for kc in range(2):
                    for c in range(4):
                        b = bank()
                        tr(ps[:, b, 0:128], xs[:, kc, c * 128:(c + 1) * 128], [XSK(kc)], [PSK(b)])
                        v, k = A(KT + c * 1280 + kc * 128, 128)
                        cp("dve" if c % 2 else "act", v, ps[:, b, 0:128], [PSK(b)], k)
                for kc in range(2):
                    vv, vk = A(VT + kc * 512, 512)
                    P.add("pool", lambda e, kc=kc, vv=vv: e.dma_start(
                        out=vv.rearrange("p (h d) -> p h d", d=64),
                        in_=cnv_d[l, :, kc * 128:(kc + 1) * 128, :].rearrange("h s d -> s h d")),
                        writes=vk, dma_sem="wq2")

            mark('attention core')
            apend = []
            LOOK = 3

            def attn_pop():
                u, hh, ptv, ptk, pa, lo, hi, vap, vk = apend.pop(0)
                hp = slice(hh * 64, hh * 64 + 64)
                n = hi - lo
                first = not u["started"][hh]
                u["started"][hh] = True
                bn, nc0 = u["num"]
                bd, dc0 = u["den"]
                mm(ps[hp, bn, nc0 + lo:nc0 + hi], vap, ptv[pa, 0:n], first, False, vk + [ptk], [PSK(bn)])
                mm(ps[hp, bd, dc0 + lo:dc0 + hi], ones[pa, 0:64], ptv[pa, 0:n], first and (bd != bn), False,
                   ["ones", ptk], [PSK(bd)])
                u["left"] -= 1
                if u["left"] == 0:
                    N = u["N"]
                    rc, rck = scr()
                    recip(rc[:, 0:N], ps[:, bd, dc0:dc0 + N], [PSK(bd)], [rck])
                    ov, ok = u["out"]
                    tt("dve", ov, ps[:, bn, nc0:nc0 + N], rc[:, 0:N], ALU.mult, [PSK(bn), rck], ok)
                    reserved.discard(bn)
                    reserved.discard(bd)

            def attn_pair(q_of, N, blocks_of, scale, out_view):
                bn = bank()
                reserved.add(bn)
                if N <= 256:
                    u = dict(num=(bn, 0), den=(bn, 256))
                else:
                    bd = bank()
                    reserved.add(bd)
                    u = dict(num=(bn, 0), den=(bd, 0))
                blist = [(hh, blk) for hh in range(2) for blk in blocks_of(hh)]
                u.update(N=N, out=out_view, left=len(blist), started=[False, False])
                qaps = [q_of(0), q_of(1)]
                for hh, blk in blist:
                    qap, qk = qaps[hh]
                    kap, kk = blk["k"]
                    pa = slice(blk["pa"][0], blk["pa"][1])
                    lo, hi = blk["qs"]
                    n = hi - lo
                    bs = bank()
                    mm(ps[pa, bs, 0:n], kap, qap[:, lo:hi], True, True, kk + qk, [PSK(bs)])
                    ptv, ptk = ptslot()
                    if blk.get("bias") is not None:
                        bap, bk = blk["bias"]
                        tmp, tk = scr()
                        stt("dve", tmp[pa, 0:n], ps[pa, bs, 0:n], scale, bap, ALU.mult, ALU.add, [PSK(bs)] + bk, [tk])
                        actf(ptv[pa, 0:n], tmp[pa, 0:n], AF.Exp, [tk], [ptk])
                    else:
                        actf(ptv[pa, 0:n], ps[pa, bs, 0:n], AF.Exp, [PSK(bs)], [ptk], scale=scale)
                    if blk.get("zero") is not None:
                        z0, z1, c0, c1 = blk["zero"]
                        P.add("dve", lambda e, ptv=ptv, z0=z0, z1=z1, c0=c0, c1=c1: e.memset(ptv[z0:z1, c0:c1], 0.0),
                              reads=[ptk], writes=[ptk])
                    vap, vk = blk["v"]
                    apend.append((u, hh, ptv, ptk, pa, lo, hi, vap, vk))
                    while len(apend) > LOOK:
                        attn_pop()

            def attn_drain():
                while apend:
                    attn_pop()

            def qna(c, lo, n):
                def f(hh):
                    v, k = A(QT + c * GT + lo, n)
                    return v[hh * 64:hh * 64 + 64, :], k
                return f

            def kna(c, hh, lo, n):
                v, k = A(KT + c * 1280 + lo, n)
                return v[hh * 64:hh * 64 + 64, :], k

            def vtile(base, h, chunk, pa):
                v, k = A(base + chunk * 512 + h * 64, 64)
                return v[pa[0]:pa[1], :], k

            if gi == 0:
                for c in range(4):
                    for s in range(4):
                        def blocks_of(hh, c=c, s=s):
                            return [dict(k=kna(c, hh, s * 256 + kb * 128, 128), pa=(0, 128),
                                         v=vtile(VT, 2 * c + hh, s * 2 + kb, (0, 128)), qs=(0, 256), bias=None)
                                    for kb in range(2)]
                        attn_pair(qna(c, s * 256, 256), 256, blocks_of, NA_SCALE,
                                  (oTn[:, c, s * 256:(s + 1) * 256], [("oTn", c, s // 2)]))
            else:
                gbuf = [bufA, bufB]
                for c in range(4):
                    for hh in range(2):
                        dma("sp", gbuf[hh][0:64, 0:960], gm_d[l, 2 * c + hh], [], GBK(hh), "ld_g%d" % hh)
                        dma("sp", gbuf[hh][64:128, 64:1024], gm_d[l, 2 * c + hh], [], GBK(hh), "ld_g%d" % hh)
                    for qt in range(2):
                        def blocks_of(hh, c=c, qt=qt):
                            bl = [dict(k=kna(c, hh, kb * 128, 128), pa=(0, 128), v=vtile(VT, 2 * c + hh, kb, (0, 128)),
                                       qs=(0, 512), bias=None) for kb in range(2)]
                            for j in range(8):
                                if j < 4:
                                    ulo, uhi = 0, 2 * j + 5
                                    rq, zp = 2 * j + 5, (0, 64)
                                else:
                                    ulo, uhi = 2 * j - 3, 15
                                    rq, zp = 2 * j - 3, (64, 128)
                                r0, r1 = max(ulo, qt * 8), min(uhi, qt * 8 + 7)
                                if r0 > r1:
                                    continue
                                i0 = 7 + r0 - 2 * j
                                n = (r1 - r0 + 1) * 64
                                zero = None
                                if r0 <= rq <= r1:
                                    zero = (zp[0], zp[1], (rq - r0) * 64, (rq - r0 + 1) * 64)
                                bl.append(dict(k=kna(c, hh, CTX + j * 128, 128), pa=(0, 128),
                                               v=vtile(VT, 2 * c + hh, 2 + j, (0, 128)),
                                               qs=((r0 - qt * 8) * 64, (r1 - qt * 8 + 1) * 64),
                                               bias=(gbuf[hh][:, i0 * 64:i0 * 64 + n], GBK(hh)), zero=zero))
                            return bl
                        attn_pair(qna(c, qt * TT, TT), TT, blocks_of, NA_SCALE,
                                  (oTn[:, c, qt * TT:(qt + 1) * TT], [("oTn", c, qt)]))

            attn_drain()
            mark('MLA')
            if gi == 1:
                dma("sp", bufA[:, :], cos_d, [], GBK(0), "ld_g0")
                dma("sp", bufB[:, :], sin_d, [], GBK(1), "ld_g1")
            def cq_cons(c):
                def f(b, t):
                    cp("act", xs[:, c, t * TT:(t + 1) * TT], ps[:, b, :], [PSK(b)], [XSK(c)])
                return f
            for c in range(2):
                proj_fm(128, cq_cons(c))
            for t in range(2):
                b = bank()
                for c in range(2):
                    sq, sqk = ptslot()
                    actf(sq, xs[:, c, t * TT:(t + 1) * TT], AF.Square, [XSK(c)], [sqk])
                    mm(ps[:, b, :], ones[:], sq, c == 0, c == 1, [sqk, "ones"], [PSK(b)])
                rs, rsk = scr()
                actf(rs, ps[:, b, :], AF.Sqrt, [PSK(b), "epsb"], [rsk], scale=1.0 / 256, bias=epsb[:])
                recip(rs, rs, [rsk], [rsk])
                for c in range(2):
                    v, k = A(CQN + c * GT + t * TT, TT)
                    stt("dve", v, xs[:, c, t * TT:(t + 1) * TT], spt[:, SP_QN + l * 2 + c:SP_QN + l * 2 + c + 1], rs,
                        ALU.mult, ALU.mult, [XSK(c), rsk, "spt"], k)

            def ckv_fm(b, t):
                ck, ckk = scr()
                cp("act", ck, ps[:, b, :], [PSK(b)], [ckk])
                sq, sqk = ptslot()
                actf(sq, ps[:, b, :], AF.Square, [PSK(b)], [sqk])
                b2 = bank()
                mm(ps[:, b2, :], ones[:], sq, True, True, [sqk, "ones"], [PSK(b2)])
                rs, rsk = scr()
                actf(rs, ps[:, b2, :], AF.Sqrt, [PSK(b2), "epsb"], [rsk], scale=1.0 / 128, bias=epsb[:])
                recip(rs, rs, [rsk], [rsk])
                v, k = A(CKVN + CTX + t * TT, TT)
                stt("dve", v, ck, spt[:, SP_KVN + l:SP_KVN + l + 1], rs, ALU.mult, ALU.mult, [ckk, rsk, "spt"], k)
            proj_fm(128, ckv_fm)

            if gi == 0:
                def kr_cons(b, t):
                    v, k = A(KRT + t * TT, TT)
                    cp("act", v[64:96, :], ps[64:96, b, :], [PSK(b)], k)
                proj_fm(96, kr_cons)
            else:
                kr_banks = {}

                def kr_cons1(b, t):
                    kr_banks[t] = b
                w3a, wka = nxt()
                w3b, wkb = nxt()
                for t in range(2):
                    ba, bb = bank(), bank()
                    for (w3, wk, bx) in ((w3a, wka, ba), (w3b, wkb, bb)):
                        for kc in range(8):
                            mm(ps[0:96, bx, :], w3[:, kc, 0:96], hT[:, kc, T0 + t * TT:T0 + (t + 1) * TT],
                               kc == 0, kc == 7, wk + [HK(kc, gi * 2 + t)], [PSK(bx)])
                    t1, t1k = scr()
                    t2, t2k = scr()
                    tt("dve", t1[64:96, :], ps[64:96, ba, :], bufA[64:96, t * TT:(t + 1) * TT], ALU.mult, [PSK(ba)] + GBK(0), [t1k])
                    tt("dve", t2[64:96, :], ps[64:96, bb, :], bufB[64:96, t * TT:(t + 1) * TT], ALU.mult, [PSK(bb)] + GBK(1), [t2k])
                    v, k = A(KRT + CTX + t * TT, TT)
                    tt("dve", v[64:96, :], t1[64:96, :], t2[64:96, :], ALU.add, [t1k, t2k], k)
                release()
                for kc in range(2):
                    sv, sk, ssem = stgslot()
                    dma("sp", sv[:, 0:128], cck_d[l, kc * 128:(kc + 1) * 128, :], [], [sk], ssem)
                    b = bank()
                    tr(ps[:, b, 0:128], sv[:, 0:128], [sk], [PSK(b)])
                    v, k = A(CKVN + kc * 128, 128)
                    cp("act", v, ps[:, b, 0:128], [PSK(b)], k)
                    sv, sk, ssem = stgslot()
                    dma("sp", sv[:, 64:96], ckr_d[l, kc * 128:(kc + 1) * 128, :], [], [sk], ssem)
                    b = bank()
                    tr(ps[0:96, b, 0:128], sv[:, 0:96], [sk], [PSK(b)])
                    v, k = A(KRT + kc * 128, 128)
                    cp("act", v[64:96, :], ps[64:96, b, 0:128], [PSK(b)], k)

            mark('v_m token-major for all heads')
            for kq in range(NK // 128):
                b = bank()
                cv, ck_ = A(CKVN + kq * 128, 128)
                mm(ps[:, b, :], cv, wuv_v, True, True, ck_ + wuv_k, [PSK(b)])
                v, k = A(VT + kq * 512, 512)
                cp("act" if kq % 2 else "dve", v, ps[:, b, :], [PSK(b)], k)

            def build_head(h):
                hh = h % 2
                for t in range(2):
                    bq = bank()
                    for kc in range(2):
                        cv, ck_ = A(CQN + kc * GT + t * TT, TT)
                        mm(ps[0:96, bq, :], wuq3[:, kc, h * 96:(h + 1) * 96], cv, kc == 0, kc == 1, wuq_k + ck_, [PSK(bq)])
                    qv, qk = A(QH[hh] + t * TT, TT)
                    if gi == 0:
                        cp("act", qv[0:96, :], ps[0:96, bq, :], [PSK(bq)], qk)
                    else:
                        bp = bank()
                        for kc in range(2):
                            cv, ck_ = A(CQN + kc * GT + t * TT, TT)
                            mm(ps[0:96, bp, :], wuqp3[:, kc, h * 96:(h + 1) * 96], cv, kc == 0, kc == 1, wuqp_k + ck_, [PSK(bp)])
                        cp("act", qv[0:64, :], ps[0:64, bq, :], [PSK(bq)], qk)
                        t1, t1k = scr()
                        t2, t2k = scr()
                        tt("dve", t1[64:96, :], ps[64:96, bq, :], bufA[64:96, t * TT:(t + 1) * TT], ALU.mult, [PSK(bq)] + GBK(0), [t1k])
                        tt("dve", t2[64:96, :], ps[64:96, bp, :], bufB[64:96, t * TT:(t + 1) * TT], ALU.mult, [PSK(bp)] + GBK(1), [t2k])
                        tt("dve", qv[64:96, :], t1[64:96, :], t2[64:96, :], ALU.add, [t1k, t2k], qk)
                lo = 0
                while lo < NK:
                    n = min(512, NK - lo)
                    b = bank()
                    cv, ck_ = A(CKVN + lo, n)
                    mm(ps[0:64, b, 0:n], wun_v[:, h * 64:(h + 1) * 64], cv, True, True, wun_k + ck_, [PSK(b)])
                    kv, kk = A(KH[hh] + lo, n)
                    cp("dve", kv[0:64, :], ps[0:64, b, 0:n], [PSK(b)], kk)
                    lo += n
                krv, krk = A(KRT, NK)
                kv, kk = A(KH[hh], NK)
                cp("act", kv[64:96, :], krv[64:96, :], krk, kk)

            for c in range(4):
                build_head(2 * c)
                build_head(2 * c + 1)

                def q_of(hh, lo, n):
                    v, k = A(QH[hh] + lo, n)
                    return v[0:96, :], k

                def kmla(hh, lo, n):
                    v, k = A(KH[hh] + lo, n)
                    return v[0:96, :], k

                if gi == 0:
                    for s in range(4):
                        def blocks_of(hh, c=c, s=s):
                            return [dict(k=kmla(hh, s * 256 + kb * 128, 128), pa=(0, 128),
                                         v=vtile(VT, 2 * c + hh, s * 2 + kb, (0, 128)), qs=(0, 256), bias=None)
                                    for kb in range(2)]
                        attn_pair(lambda hh, s=s: q_of(hh, s * 256, 256), 256, blocks_of, MLA_SCALE,
                                  (oTm[:, c, s * 256:(s + 1) * 256], [("oTm", c, s // 2)]))
                else:
                    for qt in range(2):
                        def blocks_of(hh, c=c):
                            return [dict(k=kmla(hh, kb * 128, 128), pa=(0, 128), v=vtile(VT, 2 * c + hh, kb, (0, 128)),
                                         qs=(0, 512), bias=None) for kb in range(NK // 128)]
                        attn_pair(lambda hh, qt=qt: q_of(hh, qt * TT, TT), TT, blocks_of, MLA_SCALE,
                                  (oTm[:, c, qt * TT:(qt + 1) * TT], [("oTm", c, qt)]))

            attn_drain()
            mark('short gated conv')
            nseg = 4 if gi == 0 else 1
            seglen = GT // nseg
            for j in range(4):
                cgs = []

                def cg_cons(b, t):
                    sv, sk = scr()
                    cp("act", sv, ps[:, b, :], [PSK(b)], [sk])
                    cgs.append((sv, sk))
                proj_fm(128, cg_cons)

                def xc_cons(b, t):
                    sv, sk = cgs[t]
                    tt("dve", xs[:, 0, t * TT:(t + 1) * TT], ps[:, b, :], sv, ALU.mult, [PSK(b), sk], [XSK(0)])
                proj_fm(128, xc_cons)
                cw = SP_CW + (l * 3) * 4 + j
                actf(xs[:, 1, :], xs[:, 0, :], AF.Identity, [XSK(0), "spt"], [XSK(1)],
                     scale=spt[:, cw + 4:cw + 5], bias=spt[:, SP_CB + l * 4 + j:SP_CB + l * 4 + j + 1])
                cx3 = xs[:, 0, :].rearrange("p (s n) -> p s n", n=seglen)
                t13 = xs[:, 1, :].rearrange("p (s n) -> p s n", n=seglen)
                stt("dve", t13[:, :, 1:seglen], cx3[:, :, 0:seglen - 1], spt[:, cw:cw + 1], t13[:, :, 1:seglen],
                    ALU.mult, ALU.add, [XSK(0), XSK(1), "spt"], [XSK(1)])
                stt("dve", t13[:, :, 0:seglen - 1], cx3[:, :, 1:seglen], spt[:, cw + 8:cw + 9], t13[:, :, 0:seglen - 1],
                    ALU.mult, ALU.add, [XSK(0), XSK(1), "spt"], [XSK(1)])

                def bg_cons(b, t, j=j):
                    v, k = A(YCT + j * GT + t * TT, TT)
                    tt("dve", v, ps[:, b, :], xs[:, 1, t * TT:(t + 1) * TT], ALU.mult, [PSK(b), XSK(1)], k)
                proj_fm(128, bg_cons)

            mark('gates, branch out-projections, z')
            for dc in range(8):
                for i in range(3):
                    w3, wk = nxt()
                    w3o, wko = nxt()
                    for t in range(2):
                        gt = gi * 2 + t
                        bg_ = bank()
                        for kc in range(8):
                            mm(ps[:, bg_, :], w3[:, kc, :], hT[:, kc, T0 + t * TT:T0 + (t + 1) * TT],
                               kc == 0, kc == 7, wk + [HK(kc, gt)], [PSK(bg_)])
                        bp_ = bank()
                        for kc in range(4):
                            if i == 0:
                                rv, rk = A(YCT + kc * GT + t * TT, TT)
                            elif i == 1:
                                rv, rk = oTn[:, kc, t * TT:(t + 1) * TT], [("oTn", kc, t)]
                            else:
                                rv, rk = oTm[:, kc, t * TT:(t + 1) * TT], [("oTm", kc, t)]
                            mm(ps[:, bp_, :], w3o[:, kc, :], rv, kc == 0, kc == 3, wko + rk, [PSK(bp_)])
                        tg, tgk = scr()
                        actf(tg, ps[:, bg_, :], AF.Tanh, [PSK(bg_)], [tgk], scale=0.5)
                        zacc, zak = xs[:, 0, t * TT:(t + 1) * TT], ("zacc", t)
                        if i == 0:
                            stt("dve", zacc, tg, 1.0, ps[:, bp_, :], ALU.add, ALU.mult, [tgk, PSK(bp_), XSK(0)], [zak, XSK(0)])
                        else:
                            stt("dve", tg, tg, 1.0, ps[:, bp_, :], ALU.add, ALU.mult, [tgk, PSK(bp_)], [tgk])
                            if i == 1:
                                tt("dve", zacc, zacc, tg, ALU.add, [zak, tgk, XSK(0)], [zak])
                            else:
                                zv, zk = A(ZT + dc * GT + t * TT, TT)
                                tt("dve", zv, zacc, tg, ALU.add, [zak, tgk, XSK(0)], zk)
                    release()

            mark('w_o + residual')
            for dc in range(8):
                w3, wk = nxt()
                for t in range(2):
                    gt = gi * 2 + t
                    b = bank()
                    for kc in range(8):
                        zv, zk = A(ZT + kc * GT + t * TT, TT)
                        mm(ps[:, b, :], w3[:, kc, :], zv, kc == 0, kc == 7, wk + zk, [PSK(b)])
                    xv = xT[:, dc, T0 + t * TT:T0 + (t + 1) * TT]
                    stt("dve", xv, ps[:, b, :], HGs[LC[0]][:, 1, dc, m:m + 1], xv, ALU.mult, ALU.add,
                        [PSK(b), "HG%d" % LC[0], XK(dc, gt)], [XK(dc, gt)])
                release()

        steps = []
        for l in range(L):
            steps += [("mod", l), ("norm", 0), ("ffn", l, 1), ("norm", 1), ("mix", l, 0), ("mix", l, 1),
                      ("norm", 2), ("ffn", l, 2)]
        if stage is not None:
            steps = steps[:stage]
        for si, stp in enumerate(steps):
            mark("STEP " + str(stp))
            if stp[0] == "mod":
                LC[0] = stp[1] % 2
                if stage is not None:
                    mod_compute(stp[1])
                elif stp[1] == 0:
                    mod_compute(0, nblk=6)
            elif stp[0] == "norm":
                if stp[1] == 2 and stage is None:
                    norm(2, (2, 3))
                else:
                    norm(stp[1])
            elif stp[0] == "ffn":
                extra = None
                if stp[2] == 2 and stp[1] + 1 < L and stage is None:
                    extra = mod_items_xs(stp[1] + 1)
                if stp[2] == 1 and stp[1] == 0 and stage is None:
                    extra = mod_items_xs(0, j_from=12)
                ffn(stp[1], stp[2], extra)
            else:
                mixer(stp[1], stp[2])
                if stp[2] == 0 and stage is None:
                    norm(2, (0, 1))

        mark("FINAL")
        dma("sp", bufA[:, :], fg_d, [], GBK(0), "ld_g0")
        fslots = [(xs[:, 0, :], [XSK(0)], "st_y0"), (xs[:, 1, :], [XSK(1)], "st_y1"), (bufB[:, :], GBK(1), "st_y2")]
        for q in range(16):
            t = q // 4
            sv, sk, ssem = fslots[q % 3]
            b0, b1 = bank(), bank()
            for dc in range(8):
                bb = b0 if dc < 4 else b1
                tr(ps[:, bb, (dc % 4) * 128:(dc % 4 + 1) * 128], xT[:, dc, q * 128:(q + 1) * 128], [XK(dc, t)], [PSK(bb)])
            ssq, ssqk = smslot()
            P.add("dve", lambda e, ssq=ssq: e.memset(ssq, 0.0), writes=[ssqk])
            j1, j1k = scr()
            j2, j2k = scr()
            actf(j1, ps[:, b0, :], AF.Square, [PSK(b0), ssqk], [j1k, ssqk], accum_out=ssq)
            ssq2, ssq2k = smslot()
            P.add("dve", lambda e, ssq2=ssq2: e.memset(ssq2, 0.0), writes=[ssq2k])
            actf(j2, ps[:, b1, :], AF.Square, [PSK(b1), ssq2k], [j2k, ssq2k], accum_out=ssq2)
            tt("dve", ssq, ssq, ssq2, ALU.add, [ssqk, ssq2k], [ssqk])
            actf(ssq, ssq, AF.Sqrt, [ssqk, "epsb"], [ssqk], scale=1.0 / D, bias=epsb[:])
            recip(ssq, ssq, [ssqk], [ssqk])
            stt("dve", sv[:, 0:512], ps[:, b0, :], ssq, bufA[:, 0:512], ALU.mult, ALU.mult, [PSK(b0), ssqk] + GBK(0), sk)
            stt("dve", sv[:, 512:1024], ps[:, b1, :], ssq, bufA[:, 512:1024], ALU.mult, ALU.mult, [PSK(b1), ssqk] + GBK(0), sk)
            dma("sp", y_d[q * 128:(q + 1) * 128, :], sv, sk, [], ssem)

        if stage is not None:
            dbg_h = dout("dbg_h", [128, 8, NTOK], BF16)
            dbg_m = dout("dbg_m", [128, 144], F32)
            dbg_a = dout("dbg_a", [128, 48], F32)
            dbg_g = dout("dbg_g", [128, 48], F32)
            dma("sp", dbg_h, hT[:], [HK(kc, t) for kc in range(8) for t in range(4)], [], "st_dbg")
            dma("sp", dbg_m, modTs[0][:].rearrange("p j m -> p (j m)"), [("mod", b, 0) for b in range(12)], [], "st_dbg")
            dma("sp", dbg_a, Ascs[0][:].rearrange("p a b c -> p (a b c)"), ["Asc0"], [], "st_dbg")
            dma("sp", dbg_g, HGs[0][:].rearrange("p a b c -> p (a b c)"), ["HG0"], [], "st_dbg")
        mark("END")
        import json, os
        if os.environ.get("KMARKS"):
            json.dump(marks, open(os.environ["KMARKS"], "w"))
        print("n_ops", P.n_ops, {e: len(P.ops[e]) for e in ENGS}, flush=True)
        P.emit(nc, st)
    return nc


_NC_CACHE = {}


def _host_consts():
    half, nf = 16, 8
    inv = (1.0 / (10000.0 ** (np.arange(nf, dtype=np.float32) / nf))).astype(np.float32)
    t = np.arange(GT)
    rows, cols = t // 64, t % 64
    cos_t = np.zeros((128, GT), np.float32)
    sin_t = np.zeros((128, GT), np.float32)
    for j in range(32):
        pos = rows if j < 16 else cols
        f = j % 8
        ang = pos.astype(np.float32) * inv[f]
        cos_t[64 + j] = np.cos(ang).astype(np.float32)
        sin_t[64 + j] = np.sin(ang).astype(np.float32)
    return cos_t, sin_t


def _rpb_gather(na_rpb):
    cq = np.arange(64)
    col_start = np.clip(cq - 8, 0, 48)
    ck = np.arange(64)
    rel = ck[:, None] - col_start[None, :]
    col_in = (rel >= 0) & (rel < 16)
    dc = np.clip(ck[:, None] - cq[None, :] + 15, 0, 30)
    ii = np.arange(15)
    g = na_rpb[:, :, (14 - ii)[None, :, None], dc[:, None, :]]
    g = np.where(col_in[None, None, :, None, :], g, np.float32(NEG)).astype(np.float32)
    return np.ascontiguousarray(g.reshape(L, 8, 64, 960))


def kernel(x_prompt, x_sample, cache_na_k, cache_na_v, cache_mla_ckv, cache_mla_krope, c, c_ctx,
           w_ada, b_ada, norm_g, w_ffn1_gate, w_ffn1_up, w_ffn1_down, w_ffn2_gate, w_ffn2_up, w_ffn2_down,
           w_in, conv_w, conv_b, na_rpb, mla_qnorm, w_uq, mla_kvnorm, w_ukv,
           w_conv_out, w_na_out, w_mla_out, w_o, final_g):
    f32 = lambda a: np.ascontiguousarray(np.asarray(a, dtype=np.float32))
    x_prompt, x_sample = f32(x_prompt), f32(x_sample)
    if "nc" not in _NC_CACHE:
        _NC_CACHE["nc"] = build_program()
    nc = _NC_CACHE["nc"]
    cos_t, sin_t = _host_consts()
    gm = _rpb_gather(f32(na_rpb))
    c = f32(c)
    c_ctx = f32(c_ctx)
    norm_g, b_ada = f32(norm_g), f32(b_ada)
    conv_w, conv_b = f32(conv_w), f32(conv_b)
    mla_qnorm, mla_kvnorm = f32(mla_qnorm), f32(mla_kvnorm)
    final_g_b = np.ascontiguousarray(np.broadcast_to(f32(final_g)[None, :], (128, D)))
    kvnorm_b = np.ascontiguousarray(np.broadcast_to(mla_kvnorm.reshape(1, L * 128), (128, L * 128)))
    shared = {
        "final_g_b": final_g_b, "kvnorm_b": kvnorm_b, "gm": gm, "cos_t": cos_t, "sin_t": sin_t,
        "w_ada": f32(w_ada), "w_ffn1_gate": f32(w_ffn1_gate), "w_ffn1_up": f32(w_ffn1_up), "w_ffn1_down": f32(w_ffn1_down),
        "w_ffn2_gate": f32(w_ffn2_gate), "w_ffn2_up": f32(w_ffn2_up), "w_ffn2_down": f32(w_ffn2_down),
        "w_in": f32(w_in), "w_uq": f32(w_uq), "w_ukv": f32(w_ukv), "w_conv_out": f32(w_conv_out),
        "w_na_out": f32(w_na_out), "w_mla_out": f32(w_mla_out), "w_o": f32(w_o),
    }
    in_maps = []
    for core in range(8):
        sp = np.zeros((128, SP_N), np.float32)
        cv = np.stack([c_ctx, c[core]], 0)
        sp[:, SP_C:SP_C + 16] = cv.reshape(2, 8, 128).transpose(2, 1, 0).reshape(128, 16)
        sp[:, SP_NG:SP_NG + 48] = norm_g.reshape(L, 3, 8, 128).transpose(3, 0, 1, 2).reshape(128, 48)
        sp[:, SP_BADA:SP_BADA + 144] = b_ada.reshape(L, 72, 128).transpose(2, 0, 1).reshape(128, 144)
        sp[:, SP_CW:SP_CW + 24] = conv_w.reshape(L, 3, 4, 128).transpose(3, 0, 1, 2).reshape(128, 24)
        sp[:, SP_CB:SP_CB + 8] = conv_b.reshape(L, 4, 128).transpose(2, 0, 1).reshape(128, 8)
        sp[:, SP_QN:SP_QN + 4] = mla_qnorm.reshape(L, 2, 128).transpose(2, 0, 1).reshape(128, 4)
        sp[:, SP_KVN:SP_KVN + 2] = mla_kvnorm.reshape(L, 128).transpose(1, 0)
        mp = dict(shared)
        mp["x_tok"] = np.ascontiguousarray(np.concatenate(
            [x_prompt[4 * core:4 * core + 4].reshape(GT, D), x_sample[core]], 0))
        mp["sp"] = sp
        mp["c_na_k"] = f32(cache_na_k[core])
        mp["c_na_v"] = f32(cache_na_v[core])
        mp["c_ckv"] = f32(cache_mla_ckv[core])
        mp["c_kr"] = f32(cache_mla_krope[core])
        in_maps.append(mp)
    res = run_bass_kernel_spmd(nc, in_maps, core_ids=list(range(8)))
    rs = res.results
    _NC_CACHE["last"] = rs
    y_prompt = np.concatenate([r["y"][:GT].reshape(4, 256, D) for r in rs], 0)
    y_sample = np.stack([r["y"][GT:] for r in rs], 0)
    new_k = np.concatenate([r["new_k"] for r in rs], 0)
    new_v = np.concatenate([r["new_v"] for r in rs], 0)
    new_ckv = np.concatenate([r["new_ckv"] for r in rs], 0)
    new_kr = np.concatenate([r["new_kr"] for r in rs], 0)
    return (y_prompt.astype(np.float32), y_sample.astype(np.float32), new_k.astype(np.float32),
            new_v.astype(np.float32), new_ckv.astype(np.float32), new_kr.astype(np.float32))
```
